# Optimizing a Trainium2 kernel written in Bass

```python
import jax, jax.numpy as jnp
from jax import lax
import numpy as np

D_MODEL = 2048
BATCH = 16
SEQ = 256
DEPTH = 2
DEC_BATCH = 8
DEC_SEQ = 1024
PAST_LEN = 512

GRID_W = 64
A_WIDTH = 1024
A_HEADS = 8
A_DK = A_WIDTH // A_HEADS
A_DV = A_WIDTH // A_HEADS
B_WIDTH = 1024
B_GROUPS = 4
B_CHUNK = 128
C_GROUPS = 4
SCAN_CHUNK = 64
D_FF = 5632
CONV_W = 3
N_HGRN_LAYERS = (DEPTH + 1) // 2
IN_WIDTH_0 = 5 * A_WIDTH + 2 * B_WIDTH
IN_SPLITS = (A_WIDTH, 2 * A_WIDTH, 3 * A_WIDTH, 4 * A_WIDTH, 5 * A_WIDTH, 5 * A_WIDTH + B_WIDTH)
EPS = 1e-6

kernel_name = 'hybrid_hgrn2_gmlp_fnet_convffn_diffusion_step'


def _rmsnorm(x, g):
    xf = x.astype(jnp.float32)
    y = xf * lax.rsqrt(jnp.mean(xf * xf, axis=-1, keepdims=True) + EPS)
    return (y * g.astype(jnp.float32)).astype(x.dtype)


def _gla_scan(q, k, v, logf, s0):
    bsz, n, h, _ = q.shape
    dv = v.shape[-1]
    nc = n // SCAN_CHUNK

    def blocks(t):
        return t.reshape(bsz, nc, SCAN_CHUNK, h, t.shape[-1]).transpose(1, 0, 3, 2, 4)

    q, k, v, logf = blocks(q), blocks(k), blocks(v), blocks(logf)
    b = jnp.cumsum(logf, axis=3)
    mid = SCAN_CHUNK // 2
    ref = b[:, :, :, mid - 1:mid, :]
    b_last = b[:, :, :, -1:, :]
    scores = jnp.einsum('nbhik,nbhjk->nbhij', q * jnp.exp(b - ref), k * jnp.exp(ref - b))
    lower = jnp.tril(jnp.ones((SCAN_CHUNK, SCAN_CHUNK), dtype=bool))
    scores = jnp.where(lower, scores, 0.0)
    o_intra = jnp.einsum('nbhij,nbhjv->nbhiv', scores, v)
    q_in = q * jnp.exp(b)
    k_out = k * jnp.exp(b_last - b)
    decay = jnp.exp(b_last[:, :, :, 0, :])

    def step(S, xs):
        qc, kc, vc, dc = xs
        o = jnp.einsum('bhik,bhkv->bhiv', qc, S)
        S = dc[..., None] * S + jnp.einsum('bhjk,bhjv->bhkv', kc, vc)
        return S, o

    s_final, o_inter = lax.scan(step, s0, (q_in, k_out, v, decay))
    o = (o_intra + o_inter).transpose(1, 0, 3, 2, 4).reshape(bsz, n, h, dv)
    return o, s_final


def _mixers_ab(h, w_in, lb_f, lb_b, gnorm, vnorm, ws, bs, w_out, s0):
    bsz, n, _ = h.shape
    f32 = jnp.float32
    proj = h @ w_in
    qa, fzf, fzb, ia, ga, ub, vb = jnp.split(proj, IN_SPLITS, axis=-1)

    def heads(t):
        return t.astype(f32).reshape(bsz, n, A_HEADS, -1)

    q = heads(jax.nn.silu(qa)) * (A_DK ** -0.5)
    v = heads(ia)

    def gates(fz, lb):
        lbh = lb.reshape(A_HEADS, A_DK)
        f = lbh + (1.0 - lbh) * jax.nn.sigmoid(heads(fz))
        return 1.0 - f, jnp.log(f)

    k_f, lf_f = gates(fzf, lb_f)
    k_b, lf_b = gates(fzb, lb_b)
    s0 = s0.astype(f32)
    o_fwd, s_fwd = _gla_scan(q, k_f, v, lf_f, s0[:, 0])
    o_rev, s_bwd = _gla_scan(jnp.flip(q, 1), jnp.flip(k_b, 1), jnp.flip(v, 1), jnp.flip(lf_b, 1), s0[:, 1])
    o_rec = _rmsnorm(o_fwd + jnp.flip(o_rev, 1), gnorm)
    out_a = (o_rec.reshape(bsz, n, A_WIDTH) * jax.nn.silu(ga.astype(f32))).astype(h.dtype)

    u = jax.nn.gelu(ub)
    vv = jax.nn.gelu(vb)
    nck = n // B_CHUNK
    cg = B_WIDTH // B_GROUPS
    vv = _rmsnorm(vv.reshape(bsz, nck, B_CHUNK, B_GROUPS, cg), vnorm.reshape(B_GROUPS, cg))
    mixed = jnp.einsum('gpq,bnqgc->bnpgc', ws, vv) + bs.T[:, :, None]
    out_b = u * mixed.reshape(bsz, n, B_WIDTH)

    out = jnp.concatenate([out_a, out_b], axis=-1) @ w_out
    return out, jnp.stack([s_fwd, s_bwd], axis=1)


def _fourier_mix(h):
    bsz, n, d = h.shape
    hf = h.astype(jnp.float32).reshape(bsz, n, C_GROUPS, d // C_GROUPS)
    return jnp.fft.fftn(hf, axes=(1, 3), norm='ortho').real.reshape(bsz, n, d).astype(h.dtype)


def _conv_ffn(h, w_up, cw, cb, w_down, n_rows):
    bsz, n, _ = h.shape
    up = (h @ w_up).reshape(bsz, n_rows, n // n_rows, 2 * D_FF)
    pad = jnp.pad(up, ((0, 0), (0, 0), (1, 1), (0, 0)))
    conv = pad[:, :, :-2] * cw[0] + pad[:, :, 1:-1] * cw[1] + pad[:, :, 2:] * cw[2] + cb
    gate, val = jnp.split(conv.reshape(bsz, n, 2 * D_FF), 2, axis=-1)
    return (jax.nn.silu(gate) * val) @ w_down


def setup_inputs(seed: int = 0) -> dict:
    key = jax.random.key(seed)
    ks = iter(jax.random.split(key, 48))

    def nrm(shape, scale):
        return scale * jax.random.normal(next(ks), shape, jnp.float32)

    def gain(shape):
        return 1.0 + 0.01 * jax.random.normal(next(ks), shape, jnp.float32)

    d = D_MODEL
    inp = {}
    inp['x_prompt'] = nrm((BATCH, SEQ, d), 1.0)
    inp['x_sample'] = nrm((DEC_BATCH, DEC_SEQ, d), 1.0)
    inp['state_l0_hgrn'] = nrm((DEC_BATCH, 2, A_HEADS, A_DK, A_DV), 0.5)
    inp['c'] = nrm((DEC_BATCH, d), 1.0)
    inp['c_ctx'] = nrm((d,), 1.0)
    inp['mod_w_0'] = nrm((d, 6 * d), 0.5 * d ** -0.5)
    inp['mod_b_0'] = nrm((6 * d,), 0.01)
    inp['norm1_0'] = gain((d,))
    inp['w_in_0'] = nrm((d, IN_WIDTH_0), d ** -0.5)
    inp['hgrn_lb'] = nrm((2, N_HGRN_LAYERS + 1, A_WIDTH), 1.0)
    inp['hgrn_gnorm_0'] = gain((A_DV,))
    inp['gmlp_vnorm_0'] = gain((B_WIDTH,))
    inp['gmlp_ws_0'] = nrm((B_GROUPS, B_CHUNK, B_CHUNK), B_CHUNK ** -0.5)
    inp['gmlp_bs_0'] = nrm((B_GROUPS, B_CHUNK), 0.1)
    inp['w_out_0'] = nrm((A_WIDTH + B_WIDTH, d), (A_WIDTH + B_WIDTH) ** -0.5)
    inp['norm2_0'] = gain((d,))
    inp['ffn_up_0'] = nrm((d, 2 * D_FF), d ** -0.5)
    inp['ffn_conv_w_0'] = nrm((CONV_W, 2 * D_FF), CONV_W ** -0.5)
    inp['ffn_conv_b_0'] = nrm((2 * D_FF,), 0.01)
    inp['ffn_down_0'] = nrm((D_FF, d), D_FF ** -0.5)
    inp['mod_w_1'] = nrm((d, 6 * d), 0.5 * d ** -0.5)
    inp['mod_b_1'] = nrm((6 * d,), 0.01)
    inp['norm1_1'] = gain((d,))
    inp['w_out_1'] = nrm((d, d), d ** -0.5)
    inp['norm2_1'] = gain((d,))
    inp['ffn_up_1'] = nrm((d, 2 * D_FF), d ** -0.5)
    inp['ffn_conv_w_1'] = nrm((CONV_W, 2 * D_FF), CONV_W ** -0.5)
    inp['ffn_conv_b_1'] = nrm((2 * D_FF,), 0.01)
    inp['ffn_down_1'] = nrm((D_FF, d), D_FF ** -0.5)
    inp['final_norm'] = gain((d,))
    return inp


def reference(x_prompt, x_sample, state_l0_hgrn, c, c_ctx,
              mod_w_0, mod_b_0, norm1_0, w_in_0, hgrn_lb, hgrn_gnorm_0, gmlp_vnorm_0, gmlp_ws_0, gmlp_bs_0,
              w_out_0, norm2_0, ffn_up_0, ffn_conv_w_0, ffn_conv_b_0, ffn_down_0,
              mod_w_1, mod_b_1, norm1_1, w_out_1, norm2_1, ffn_up_1, ffn_conv_w_1, ffn_conv_b_1, ffn_down_1,
              final_norm):
    layers = [
        {'mod_w': mod_w_0, 'mod_b': mod_b_0, 'norm1': norm1_0, 'norm2': norm2_0, 'up': ffn_up_0,
         'cw': ffn_conv_w_0, 'cb': ffn_conv_b_0, 'down': ffn_down_0, 'w_in': w_in_0, 'gnorm': hgrn_gnorm_0,
         'vnorm': gmlp_vnorm_0, 'ws': gmlp_ws_0, 'bs': gmlp_bs_0, 'w_out': w_out_0},
        {'mod_w': mod_w_1, 'mod_b': mod_b_1, 'norm1': norm1_1, 'norm2': norm2_1, 'up': ffn_up_1,
         'cw': ffn_conv_w_1, 'cb': ffn_conv_b_1, 'down': ffn_down_1, 'w_out': w_out_1},
    ]
    lb_all = jnp.cumsum(jax.nn.softmax(hgrn_lb.astype(jnp.float32), axis=1), axis=1)

    def run_trunk(x, cond, s0_list, n_rows):
        states = []
        for l in range(DEPTH):
            p = layers[l]
            mod = (jax.nn.silu(cond) @ p['mod_w'] + p['mod_b'])[:, None, :]
            sh1, sc1, g1, sh2, sc2, g2 = jnp.split(mod, 6, axis=-1)
            h = _rmsnorm(x, p['norm1']) * (1.0 + sc1) + sh1
            if l % 2 == 0:
                j = l // 2
                mix, s = _mixers_ab(h, p['w_in'], lb_all[0, j], lb_all[1, j], p['gnorm'], p['vnorm'],
                                    p['ws'], p['bs'], p['w_out'], s0_list[j])
                states.append(s)
            else:
                mix = _fourier_mix(h) @ p['w_out']
            x = x + g1 * mix
            h = _rmsnorm(x, p['norm2']) * (1.0 + sc2) + sh2
            x = x + g2 * _conv_ffn(h, p['up'], p['cw'], p['cb'], p['down'], n_rows)
        return _rmsnorm(x, final_norm), states

    zero_state = jnp.zeros((x_prompt.shape[0], 2, A_HEADS, A_DK, A_DV), jnp.float32)
    y_prompt, ctx_states = run_trunk(x_prompt, c_ctx[None, :], [zero_state] * N_HGRN_LAYERS, 1)
    state_l0_hgrn_new = ctx_states[0].astype(x_prompt.dtype)

    n_rows = x_sample.shape[1] // GRID_W
    y_sample, _ = run_trunk(x_sample, c, [state_l0_hgrn], n_rows)
    return (y_prompt, y_sample, state_l0_hgrn_new)
```

```python
import contextlib
import os
import numpy as np
import concourse.bass as bass
import concourse.mybir as mybir
from concourse.bass_utils import run_bass_kernel_spmd

F32 = mybir.dt.float32
BF16 = mybir.dt.bfloat16
AF = mybir.ActivationFunctionType
ALU = mybir.AluOpType
AX = mybir.AxisListType

NCORES = 8
D = 2048
DC = 16
DFF = 5632
NPAIR = 44
NTOK = 1536
EPS = 1e-6
A_HEADS = 8
JOBS = [
    dict(name="P", t0=0, nt=512, seqs=[(0, 256), (256, 256)], row=256, cond=0),
    dict(name="S", t0=512, nt=1024, seqs=[(0, 1024)], row=64, cond=1),
]
STAGE = int(os.environ.get("MK_STAGE", "99"))


class Op:
    __slots__ = ("id", "eng", "fn", "deps", "dmakey", "ms", "semval", "phase")


class Prog:
    ENGS = ("pe", "act", "dve", "pool", "sp")
    CH = 4000

    def __init__(self):
        self.ops = []
        self.lastw = {}
        self.rd_eng = {}
        self.rd_dma = {}
        self.bank = 0
        self.out_keys = set()
        self.bar = set()
        self.phase = "init"
        self.scopes = bool(int(os.environ.get("MK_SCOPES", "0")))

    def add(self, eng, fn, reads=(), writes=(), dmakey=None):
        op = Op()
        op.id = len(self.ops)
        op.eng = eng
        op.fn = fn
        op.dmakey = dmakey
        op.ms = None
        op.semval = None
        op.phase = self.phase
        deps = set()
        for r in reads:
            w = self.lastw.get(r)
            if w is not None:
                deps.add(w)
        for w_ in writes:
            w = self.lastw.get(w_)
            if w is not None:
                deps.add(w)
            for v in self.rd_eng.get(w_, {}).values():
                deps.add(v)
            for v in self.rd_dma.get(w_, ()):
                deps.add(v)
        deps |= self.bar
        op.deps = deps
        for r in reads:
            if dmakey is not None:
                self.rd_dma.setdefault(r, []).append(op.id)
            else:
                self.rd_eng.setdefault(r, {})[eng] = op.id
        for w_ in writes:
            self.lastw[w_] = op.id
            self.rd_eng[w_] = {}
            self.rd_dma[w_] = []
        self.ops.append(op)
        return op

    def pe(self, fn, reads=(), writes=()):
        return self.add("pe", fn, reads, writes)

    def act(self, fn, reads=(), writes=()):
        return self.add("act", fn, reads, writes)

    def dve(self, fn, reads=(), writes=()):
        return self.add("dve", fn, reads, writes)

    def dma(self, q, fn, key, reads=(), writes=(), out=False):
        if out:
            self.out_keys.add(key)
        return self.add(q, fn, reads, writes, dmakey=key)

    def barrier(self):
        last = {}
        for op in self.ops:
            last[op.eng if op.dmakey is None else ("d", op.dmakey)] = op.id
        self.bar = set(last.values())

    def next_bank(self):
        b = self.bank
        self.bank = (self.bank + 1) % 8
        return b

    def emit(self, nc):
        ops = self.ops
        by_eng = {e: [] for e in self.ENGS}
        for op in ops:
            by_eng[op.eng].append(op)
        needed = set()
        for op in ops:
            for d in op.deps:
                dop = ops[d]
                if dop.dmakey is None and not (dop.eng == "pe" and op.eng == "pe"):
                    needed.add(d)
        nsem = {}
        for e in ("pe", "act", "dve"):
            cnt = 0
            for op in by_eng[e]:
                if op.id in needed:
                    op.ms = cnt
                    cnt += 1
            nsem[e] = (cnt + self.CH - 1) // self.CH
        keycnt = {}
        for op in ops:
            if op.dmakey is not None:
                keycnt[op.dmakey] = keycnt.get(op.dmakey, 0) + 1
                op.semval = 16 * keycnt[op.dmakey]
        with contextlib.ExitStack() as es:
            engsem = {e: [es.enter_context(nc.semaphore(f"s_{e}{i}")) for i in range(max(1, nsem[e]))]
                      for e in ("pe", "act", "dve")}
            dmasem = {k: es.enter_context(nc.semaphore(f"d_{k}")) for k in keycnt}
            block = es.enter_context(nc.Block())

            def body(h, e):
                waited = {}
                for op in by_eng[e]:
                    for d in sorted(op.deps):
                        dop = ops[d]
                        if dop.dmakey is not None:
                            sk = ("d", dop.dmakey)
                            sem = dmasem[dop.dmakey]
                            val = dop.semval
                        else:
                            if dop.eng == "pe" and e == "pe":
                                continue
                            ci = dop.ms // self.CH
                            sk = (dop.eng, ci)
                            sem = engsem[dop.eng][ci]
                            val = dop.ms % self.CH + 1
                        if waited.get(sk, 0) >= val:
                            continue
                        h.wait_ge(sem, val)
                        waited[sk] = val
                    if self.scopes:
                        with nc.named_scope(op.phase):
                            ins = op.fn(h)
                    else:
                        ins = op.fn(h)
                    if op.dmakey is not None:
                        ins.then_inc(dmasem[op.dmakey], 16)
                    elif op.id in needed:
                        ins.then_inc(engsem[e][op.ms // self.CH], 1)
                if e == "sp":
                    for k in sorted(self.out_keys):
                        h.wait_ge(dmasem[k], 16 * keycnt[k])

            @block.tensor
            def _(t):
                body(t, "pe")

            @block.scalar
            def _(a):
                body(a, "act")

            @block.vector
            def _(v):
                body(v, "dve")

            @block.gpsimd
            def _(g):
                body(g, "pool")

            @block.sync
            def _(s):
                body(s, "sp")


class SB:
    def __init__(self, nc):
        self.nc = nc
        self.n = 0

    def at(self, off, shape, dt):
        self.n += 1
        return self.nc.alloc_sbuf_tensor_at(f"t{self.n}", shape, dt, offset=off + 16512)


def build_program():
    nc = bass.Bass("TRN2", target_bir_lowering=False)
    pr = Prog()

    def din(name, shape, dt=F32):
        return nc.dram_tensor(name, list(shape), dt, kind="ExternalInput").ap()

    def dout(name, shape, dt=F32):
        return nc.dram_tensor(name, list(shape), dt, kind="ExternalOutput").ap()

    xT_d = din("xT", [D, NTOK])
    yT_d = dout("yT", [D, NTOK])
    st_in = din("st_in", [2, A_HEADS, 128, 128])
    st_out = dout("st_out", [2, 2, A_HEADS, 128, 128])
    condT = din("condT", [128, DC, 2])
    small = din("small", [128, 1024])
    modw = [din(f"mod_w_r_{l}", [6 * D // 256, 128, DC, 256]) for l in range(2)]
    wtok_r = din("wtok_r", [A_HEADS, 128, DC, 384])
    wfm_r = din("wfm_r", [A_HEADS, 128, DC, 256])
    wvu_r = din("wvu_r", [4, 128, DC, 512])
    w_out0 = din("w_out_0", [D, D])
    w_out1 = din("w_out_1", [D, D])
    ffn_up = [din(f"ffn_up_r_{l}", [NPAIR, 128, DC, 256]) for l in range(2)]
    ffn_dn = [din(f"ffn_down_{l}", [DFF, D]) for l in range(2)]
    lb_b = din("lb_b", [A_HEADS, 128, 2, 2, 128])
    vn_b = din("vn_b", [128, 1024])
    bs_b = din("bs_b", [128, 4, 512])
    wsT = din("wsT", [128, 4, 128])
    cst = din("cst", [128, 1024])
    dftc = din("dftc", [512, 2, 512])
    dft256 = din("dft256", [256, 2, 256])
    dft1024 = din("dft1024", [1024, 2, 1024])

    DBG = bool(int(os.environ.get("MK_DEBUG", "0")))
    dbg_names = []

    def dbg(name, ap, shape, dt, reads):
        if not DBG:
            return
        d = dout("dbg_" + name, shape, dt)
        dbg_names.append("dbg_" + name)
        pr.dma("sp", lambda e: e.dma_start(out=d, in_=ap), key="dbg_" + name, reads=reads, out=True)

    sb = SB(nc)
    xT = sb.at(0, [128, DC, 1024], F32)
    hT = sb.at(65536, [128, DC, 1024], BF16)
    o = 98304
    smallt = sb.at(o, [128, 1024], F32); o += 4096
    modT = [sb.at(o + l * 768, [128, 96, 2], F32) for l in range(2)]; o += 1536
    aT = sb.at(o, [128, 2, 2, 2, DC], F32); o += 512
    cond32 = sb.at(o, [128, DC, 2], F32); o += 128
    scT = sb.at(o, [128, DC, 2], BF16); o += 64
    ones_b = sb.at(o, [128, 128], BF16); o += 256
    ident_b = sb.at(o, [128, 128], BF16); o += 256
    rstd = sb.at(o, [128, 512], F32); o += 2048
    ntmp = [sb.at(o + i * 2048, [128, 512], F32) for i in range(2)]; o += 4096
    cstt = sb.at(o, [128, 1024], F32); o += 4096
    SCR = o
    assert SCR <= 120000, SCR
    LIM = 212800

    ps_all = nc.alloc_psum_tensor("ps_all", [128, 8 * 512], F32)

    def psb(b, n=512):
        return ps_all[:, b * 512:b * 512 + n]

    def sm(off, n):
        return smallt[:, off:off + n]

    SM_MODB = [0, 96]
    SM_GAIN = 192
    SM_CW = [272, 272 + 352]
    SM_GN = 976
    SM_EPS = 977
    SM_ONE = 978

    pr.dma("sp", lambda e: e.dma_start(out=smallt[:], in_=small[:, :]), key="small", writes=["small"])
    pr.dma("sp", lambda e: e.dma_start(out=cond32[:], in_=condT[:, :, :]), key="cond", writes=["cond32"])
    pr.dma("sp", lambda e: e.dma_start(out=cstt[:], in_=cst[:, :]), key="cst", writes=["cst"])
    pr.dve(lambda e: e.memset(ones_b[:], 1.0), writes=["ones"])
    pr.dve(lambda e: e.tensor_copy(ident_b[:], cstt[:, 0:128]), reads=["cst"], writes=["ident"])
    pr.act(lambda e: e.activation(out=scT[:], in_=cond32[:], func=AF.Silu), reads=["cond32"], writes=["scT"])

    pr.phase = "mod"
    CW = 256
    NMC = 6 * D // CW
    MWOFF = 212800 - 2 * 8192
    mw = [sb.at(MWOFF + i * 8192, [128, DC, CW], BF16) for i in range(2)]
    mod_state = {"n": 0}

    def mod_chunk(l, ci):
        slot = mod_state["n"] % 2
        mod_state["n"] += 1
        pr.dma("pool", lambda e: e.dma_start(out=mw[slot][:], in_=modw[l][ci]), key=f"mw{slot}", writes=[f"mw{slot}"])
        b = pr.next_bank()
        nb = CW // 128

        def mm(e):
            ins = None
            for j in range(nb):
                for k in range(DC):
                    ins = e.matmul(psb(b)[:, 2 * j:2 * j + 2], mw[slot][:, k, j * 128:(j + 1) * 128],
                                   scT[:, k, :], start=(k == 0), stop=(k == DC - 1))
            return ins
        pr.pe(mm, reads=[f"mw{slot}", "scT"], writes=[f"ps{b}"])
        c0 = ci * nb
        for r in range(2):
            pr.dve(lambda e, r=r: e.tensor_tensor(out=modT[l][:, c0:c0 + nb, r], in0=psb(b)[:, r:2 * nb:2],
                                                  in1=sm(SM_MODB[l] + c0, nb), op=ALU.add),
                   reads=[f"ps{b}", "small"], writes=[f"modT{l}"])

    def mod_derived(l, wh):
        for r in range(2):
            sc_ap = modT[l][:, (1 + 3 * wh) * 16:(2 + 3 * wh) * 16, r]
            g_ap = sm(SM_GAIN + (2 * l + wh) * 16, 16)
            pr.dve(lambda e, sc_ap=sc_ap, g_ap=g_ap, r=r: e.scalar_tensor_tensor(
                out=aT[:, l, wh, r, :], in0=sc_ap, scalar=1.0, in1=g_ap, op0=ALU.add, op1=ALU.mult),
                reads=[f"modT{l}", "small"], writes=["aT"])

    for ci in range(2 * D // CW):
        mod_chunk(0, ci)
    mod_derived(0, 0)
    mod_todo = [(0, ci) for ci in range(2 * D // CW, NMC)] + [(1, ci) for ci in range(NMC)]

    def mod_ap(l, idx, r, c):
        return modT[l][:, idx * 16 + c, r:r + 1]

    def norm_tile(job, tt, scale_fn, bias_fn, out_fn, out_res, out_dt_is_f32=False):
        ts = slice(tt * 512, (tt + 1) * 512)
        pr.act(lambda e: e.activation(out=hT[:, :, ts], in_=xT[:, :, ts], func=AF.Square),
               reads=[f"xT{tt}"], writes=[f"hT{tt}"])
        b = pr.next_bank()

        def mm(e):
            ins = None
            for k in range(DC):
                ins = e.matmul(psb(b), ones_b[:], hT[:, k, ts], start=(k == 0), stop=(k == DC - 1))
            return ins
        pr.pe(mm, reads=[f"hT{tt}", "ones"], writes=[f"ps{b}"])
        pr.act(lambda e: e.activation(out=rstd[:], in_=psb(b), func=AF.Sqrt, bias=sm(SM_EPS, 1), scale=1.0 / D),
               reads=[f"ps{b}", "small"], writes=["rstd"])
        pr.dve(lambda e: e.reciprocal(rstd[:], rstd[:]), reads=["rstd"], writes=["rstd"])
        for c in range(DC):
            t = ntmp[c % 2]
            tn = f"ntmp{c % 2}"
            pr.dve(lambda e, c=c, t=t: e.tensor_tensor(out=t[:], in0=xT[:, c, ts], in1=rstd[:], op=ALU.mult),
                   reads=[f"xT{tt}", "rstd"], writes=[tn])
            bi = bias_fn(c)
            if bi is None:
                pr.act(lambda e, c=c, t=t: e.activation(out=out_fn(c), in_=t[:], func=AF.Copy, scale=scale_fn(c)),
                       reads=[tn, "small", "aT", "modT0", "modT1"], writes=[out_res])
            else:
                pr.act(lambda e, c=c, t=t, bi=bi: e.activation(out=out_fn(c), in_=t[:], func=AF.Identity,
                                                               bias=bi, scale=scale_fn(c)),
                       reads=[tn, "small", "aT", "modT0", "modT1"], writes=[out_res])

    def adaln(job, l, wh):
        r = job["cond"]
        for tt in range(job["nt"] // 512):
            ts = slice(tt * 512, (tt + 1) * 512)
            norm_tile(job, tt,
                      scale_fn=lambda c: aT[:, l, wh, r, c:c + 1],
                      bias_fn=lambda c: mod_ap(l, 3 * wh, r, c),
                      out_fn=lambda c, ts=ts: hT[:, c, ts], out_res=f"hT{tt}")

    def accum_x(job, b, dc, tt, gate_ap):
        ts = slice(tt * 512, (tt + 1) * 512)
        pr.dve(lambda e: e.scalar_tensor_tensor(out=xT[:, dc, ts], in0=psb(b), scalar=gate_ap, in1=xT[:, dc, ts],
                                                op0=ALU.mult, op1=ALU.add),
               reads=[f"ps{b}", f"xT{tt}", "modT0", "modT1"], writes=[f"xT{tt}"])


    def gelu_tanh(src_ps, n, dst, xs, wk, res_ps, res_dst):
        pr.act(lambda e: e.activation(out=xs, in_=src_ps, func=AF.Copy), reads=[res_ps], writes=["g_xs"])
        pr.dve(lambda e: e.tensor_tensor(out=wk, in0=xs, in1=xs, op=ALU.mult), reads=["g_xs"], writes=["g_wk"])
        pr.dve(lambda e: e.tensor_scalar(out=wk, in0=wk, scalar1=0.044715, scalar2=1.0, op0=ALU.mult, op1=ALU.add),
               reads=["g_wk"], writes=["g_wk"])
        pr.dve(lambda e: e.tensor_tensor(out=wk, in0=wk, in1=xs, op=ALU.mult), reads=["g_wk", "g_xs"], writes=["g_wk"])
        pr.act(lambda e: e.activation(out=wk, in_=wk, func=AF.Sigmoid, scale=1.5957691216057308),
               reads=["g_wk"], writes=["g_wk"])
        pr.dve(lambda e: e.tensor_tensor(out=dst, in0=wk, in1=xs, op=ALU.mult), reads=["g_wk", "g_xs"], writes=[res_dst])

    def wout0_partial(job, w0s, MIX, row0, w0ress, mixres="MIX"):
        r = job["cond"]
        ntile = job["nt"] // 512
        for dcg in range(4):
            src = w_out0[row0:row0 + 1024, dcg * 512:(dcg + 1) * 512].rearrange("(k p) f -> p k f", p=128)
            w0 = w0s[dcg % len(w0s)]
            w0res = w0ress[dcg % len(w0s)]
            pr.dma("pool", lambda e, src=src, w0=w0: e.dma_start(out=w0[:], in_=src), key=w0res, writes=[w0res])
            for dcl in range(4):
                dc = dcg * 4 + dcl
                for tt in range(ntile):
                    ts = slice(tt * 512, (tt + 1) * 512)
                    b = pr.next_bank()

                    def mm(e, dcl=dcl, ts=ts, b=b, w0=w0):
                        ins = None
                        for k in range(8):
                            ins = e.matmul(psb(b), w0[:, k, dcl * 128:(dcl + 1) * 128], MIX[:, k, ts], start=(k == 0), stop=(k == 7))
                        return ins
                    pr.pe(mm, reads=[w0res, mixres], writes=[f"ps{b}"])
                    accum_x(job, b, dc, tt, mod_ap(0, 2, r, dc))

    def mixer0(job, pre=None):
        pr.barrier()
        pr.phase = f"mixA_{job['name']}"
        r = job["cond"]
        nt = job["nt"]
        ntile = nt // 512
        NB = nt // 128
        jn = job["name"]
        nch = 2 * len(job["seqs"])
        o = SCR
        wtok = sb.at(o, [128, DC, 384], BF16)
        w0 = sb.at(o, [128, 8, 512], BF16); o += 12288
        wfm = sb.at(o, [128, DC, 256], BF16)
        w0x = sb.at(o, [128, 8, 512], BF16); o += 8192
        sets = []
        for i_ in range(2):
            X = {}
            X["KT"] = sb.at(o, [128, NB, 256], BF16); o += NB * 512
            X["KTT"] = sb.at(o, [128, 2, nt], BF16); o += nt * 4
            X["QT"] = sb.at(o, [128, 2, nt], BF16); o += nt * 4
            X["V0"] = sb.at(o, [128, NB, 128], BF16); o += NB * 256
            X["V1"] = sb.at(o, [128, NB, 128], BF16); o += NB * 256
            X["SCAL"] = sb.at(o, [128, NB, 2, 6], F32); o += NB * 48
            X["GS"] = sb.at(o, [128, nt], BF16); o += nt * 2
            X["oml"] = sb.at(o, [128, 2, 128], F32); o += 1024
            sets.append(X)
        lbt = sb.at(o, [128, 2, 2, 128], F32); o += 2048
        QS = sb.at(o, [128, nt], BF16); o += nt * 2
        LFr = sb.at(o, [128, 2, 256], F32); o += 2048
        O = sb.at(o, [128, nt], F32); o += nt * 4
        EA = [sb.at(o + i_ * 512, [128, 128], F32) for i_ in range(2)]; o += 1024
        BT = sb.at(o, [128, 256], F32)
        sgt = sb.at(o, [128, 256], F32); o += 1024
        kf = [sb.at(o + i_ * 1024, [128, 256], F32) for i_ in range(2)]; o += 2048
        S = [sb.at(o + i_ * 512, [128, 128], F32) for i_ in range(nch)]; o += nch * 512
        S2 = [sb.at(o + i_ * 512, [128, 128], F32) for i_ in range(nch)]; o += nch * 512
        RING = 4
        SBF = [[sb.at(o + (i_ * RING + k_) * 256, [128, 128], BF16) for k_ in range(RING)] for i_ in range(nch)]; o += nch * RING * 256
        STm = [[sb.at(o + (i_ * RING + k_) * 128, [128, 64], BF16) for k_ in range(RING)] for i_ in range(nch)]; o += nch * RING * 128
        sq = sb.at(o, [128, 512], BF16); o += 1024
        t1 = ntmp[0]
        rs = rstd
        MIX = sb.at(o, [128, 8, nt], BF16); o += nt * 16
        assert o <= (MWOFF if jn == "P" else LIM), o

        def prep(h):
            sx = h % 2
            X = sets[sx]
            KT, KTT, QT, V0, V1, SCAL, GS, oml = (X[k_] for k_ in ("KT", "KTT", "QT", "V0", "V1", "SCAL", "GS", "oml"))
            pr.dma("sp", lambda e: e.dma_start(out=lbt[:], in_=lb_b[h]), key="lbt", writes=["lbt"])
            pr.dve(lambda e: e.tensor_tensor(out=oml[:], in0=lbt[:, :, 0, :], in1=lbt[:, :, 1, :], op=ALU.subtract),
                   reads=["lbt"], writes=[f"oml{sx}"])
            pr.act(lambda e: e.activation(out=oml[:], in_=oml[:], func=AF.Sigmoid, scale=-1.0), reads=[f"oml{sx}"], writes=[f"oml{sx}"])
            pr.dma("pool", lambda e: e.dma_start(out=wfm[:], in_=wfm_r[h]), key="wfm", writes=["wfm"])
            pr.dma("pool", lambda e: e.dma_start(out=wtok[:], in_=wtok_r[h]), key="wtok", writes=["wtok"])
            yield
            for tt in range(ntile):
                ts = slice(tt * 512, (tt + 1) * 512)
                for i, (dst, dn_) in enumerate(((QS, "QS"), (GS, f"GS{sx}"))):
                    b = pr.next_bank()

                    def mm(e, i=i, ts=ts, b=b):
                        ins = None
                        for k in range(DC):
                            ins = e.matmul(psb(b), wfm[:, k, i * 128:(i + 1) * 128], hT[:, k, ts], start=(k == 0), stop=(k == DC - 1))
                        return ins
                    pr.pe(mm, reads=["wfm", f"hT{tt}"], writes=[f"ps{b}"])
                    pr.act(lambda e, dst=dst, ts=ts, b=b: e.activation(out=dst[:, ts], in_=psb(b), func=AF.Silu),
                           reads=[f"ps{b}"], writes=[dn_])
                    yield

            def decays(b):
                tk = slice(b * 128, (b + 1) * 128)
                q = b % 2
                ls = b % 2
                bk = pr.next_bank()

                def mm_tok(e, bk=bk):
                    ins = None
                    for dr in range(2):
                        ins = e.matmul(psb(bk)[:, dr * 128:(dr + 1) * 128], cstt[:, 396 + dr * 128:396 + (dr + 1) * 128],
                                       LFr[:, ls, dr * 128:(dr + 1) * 128], start=True, stop=True)
                    return ins
                pr.pe(mm_tok, reads=["cst", f"LF{ls}"], writes=[f"ps{bk}"])
                pr.act(lambda e, bk=bk: e.activation(out=BT[:], in_=psb(bk)[:, 0:256], func=AF.Exp), reads=[f"ps{bk}"], writes=["sgt"])
                pr.dve(lambda e: e.tensor_tensor(out=KT[:, b, :], in0=kf[q][:], in1=BT[:], op=ALU.mult),
                       reads=[f"kf{q}", "sgt"], writes=[f"KT{sx}"])
                decays_ea(b)

            def transposes(b):
                tk = slice(b * 128, (b + 1) * 128)
                bt = pr.next_bank()

                def tr(e, bt=bt):
                    ins = None
                    pv = ps_all[:, bt * 512:(bt + 1) * 512].bitcast(BF16)
                    for dr in range(2):
                        ins = e.transpose(pv[:, dr * 128:(dr + 1) * 128], KT[:, b, dr * 128:(dr + 1) * 128], ident_b[:])
                    return ins
                pr.pe(tr, reads=[f"KT{sx}", "ident"], writes=[f"ps{bt}"])

                def cpk(e, bt=bt):
                    pv = ps_all[:, bt * 512:(bt + 1) * 512].bitcast(BF16)
                    return e.tensor_copy(KTT[:, :, tk], pv[:, 0:256].rearrange("p (d t) -> p d t", d=2))
                pr.dve(cpk, reads=[f"ps{bt}"], writes=[f"KTT{sx}"])

            def decays_ea(b):
                tk = slice(b * 128, (b + 1) * 128)
                ls = b % 2
                for dr in range(2):
                    be = pr.next_bank()
                    pr.pe(lambda e, dr=dr, be=be: e.matmul(psb(be)[:, 0:134], LFr[:, ls, dr * 128:(dr + 1) * 128],
                                                           cstt[:, 128 + dr * 134:128 + (dr + 1) * 134], start=True, stop=True),
                          reads=["cst", f"LF{ls}"], writes=[f"ps{be}"])
                    pr.act(lambda e, dr=dr, be=be: e.activation(out=EA[dr][:], in_=psb(be)[:, 0:128], func=AF.Exp),
                           reads=[f"ps{be}"], writes=[f"EA{dr}"])
                    pr.act(lambda e, dr=dr, be=be: e.activation(out=SCAL[:, b, dr, :], in_=psb(be)[:, 128:134], func=AF.Exp),
                           reads=[f"ps{be}"], writes=[f"SCAL{sx}"])
                    pr.dve(lambda e, dr=dr: e.scalar_tensor_tensor(out=QT[:, dr, tk], in0=QS[:, tk], scalar=float(128 ** -0.5),
                                                                   in1=EA[dr][:], op0=ALU.mult, op1=ALU.mult),
                           reads=["QS", f"EA{dr}"], writes=[f"QT{sx}"])

            for b in range(NB):
                tk = slice(b * 128, (b + 1) * 128)
                tt = b // 4
                q = b % 2
                ls = b % 2
                bk = pr.next_bank()

                def mm(e, tk=tk, bk=bk):
                    ins = None
                    for k in range(DC):
                        ins = e.matmul(psb(bk)[:, 0:384], hT[:, k, tk], wtok[:, k, :], start=(k == 0), stop=(k == DC - 1))
                    return ins
                pr.pe(mm, reads=["wtok", f"hT{tt}"], writes=[f"ps{bk}"])
                pr.act(lambda e, bk=bk: e.activation(out=sgt[:], in_=psb(bk)[:, 0:256], func=AF.Sigmoid, scale=-1.0),
                       reads=[f"ps{bk}"], writes=["sgt"])
                pr.dve(lambda e, q=q: e.tensor_tensor(out=kf[q][:], in0=sgt[:], in1=oml[:].rearrange("p d k -> p (d k)"), op=ALU.mult),
                       reads=["sgt", f"oml{sx}"], writes=[f"kf{q}"])
                pr.act(lambda e, q=q, ls=ls: e.activation(out=LFr[:, ls, :], in_=kf[q][:], func=AF.Ln, bias=sm(SM_ONE, 1), scale=-1.0),
                       reads=[f"kf{q}", "small"], writes=[f"LF{ls}"])
                pr.dve(lambda e, b=b, bk=bk: e.tensor_copy(V0[0:64, b, :], psb(bk)[0:64, 256:384]), reads=[f"ps{bk}"], writes=[f"V{sx}"])
                pr.act(lambda e, b=b, bk=bk: e.activation(out=V1[64:128, b, :], in_=psb(bk)[64:128, 256:384], func=AF.Copy),
                       reads=[f"ps{bk}"], writes=[f"V{sx}"])
                if b >= 1:
                    decays(b - 1)
                if b >= 2:
                    transposes(b - 2)
                yield
            decays(NB - 1)
            transposes(NB - 2)
            transposes(NB - 1)
            yield

        def scan(h):
            sx = h % 2
            X = sets[sx]
            KT, KTT, QT, SCAL, GS = (X[k_] for k_ in ("KT", "KTT", "QT", "SCAL", "GS"))
            Vm = (X["V0"], X["V1"])
            chains = [(si, dr) for si in range(len(job["seqs"])) for dr in range(2)]
            pr.dve(lambda e: e.memset(O[:], 0.0), writes=["O"])
            for ci, (si, dr) in enumerate(chains):
                if jn == "S":
                    pr.dma("sp", lambda e, ci=ci, dr=dr: e.dma_start(out=S[ci][:], in_=st_in[dr, h, :, :]),
                           key=f"stin{ci}", writes=[f"S{ci}"])
                else:
                    pr.dve(lambda e, ci=ci: e.memset(S[ci][:], 0.0), writes=[f"S{ci}"])
            L = job["seqs"][0][1]
            nchunk = L // 64
            LAG = 2

            def geom(ci, step):
                si, dr = chains[ci]
                s0 = job["seqs"][si][0]
                c = step if dr == 0 else nchunk - 1 - step
                tok0 = s0 + c * 64
                b = tok0 // 128
                hf = (tok0 % 128) // 64
                return dr, b, hf, slice(tok0, tok0 + 64), slice(b * 128, (b + 1) * 128)

            def stage_a(ci, step):
                dr, b, hf, csl, tkb = geom(ci, step)
                kr = step % RING
                bs_ = pr.next_bank()
                pr.pe(lambda e: e.matmul(psb(bs_)[:, 0:64], KTT[:, dr, tkb], QT[:, dr, csl], start=True, stop=True),
                      reads=[f"KTT{sx}", f"QT{sx}"], writes=[f"ps{bs_}"])
                mo = 652 + (hf * 2 + dr) * 64
                pr.dve(lambda e: e.copy_predicated(out=STm[ci][kr][:], mask=cstt[:, mo:mo + 64].bitcast(mybir.dt.uint32),
                                                   data=psb(bs_)[:, 0:64]),
                       reads=[f"ps{bs_}", "cst"], writes=[f"ST{ci}_{kr}"])
                bp = pr.next_bank()
                pr.pe(lambda e: e.matmul(psb(bp)[:, 0:128], KT[:, b, dr * 128:(dr + 1) * 128], Vm[hf][:, b, :], start=True, stop=True),
                      reads=[f"KT{sx}", f"V{sx}"], writes=[f"ps{bp}"])
                return bp

            def stage_b(ci, step, bp):
                dr, b, hf, csl, tkb = geom(ci, step)
                kr = step % RING
                sc0 = SCAL[:, b, dr, 3 * hf + 0:3 * hf + 1]
                sc1 = SCAL[:, b, dr, 3 * hf + 1:3 * hf + 2]
                sc2 = SCAL[:, b, dr, 3 * hf + 2:3 * hf + 3]
                pr.act(lambda e: e.activation(out=SBF[ci][kr][:], in_=S[ci][:], func=AF.Copy, scale=sc0),
                       reads=[f"S{ci}", f"SCAL{sx}"], writes=[f"SBF{ci}_{kr}"])
                pr.act(lambda e: e.activation(out=S2[ci][:], in_=S[ci][:], func=AF.Copy, scale=sc1),
                       reads=[f"S{ci}", f"SCAL{sx}"], writes=[f"S2{ci}"])
                pr.dve(lambda e: e.scalar_tensor_tensor(out=S[ci][:], in0=psb(bp)[:, 0:128], scalar=sc2, in1=S2[ci][:],
                                                        op0=ALU.mult, op1=ALU.add),
                       reads=[f"ps{bp}", f"S2{ci}", f"SCAL{sx}"], writes=[f"S{ci}"])

            def stage_c(ci, step):
                dr, b, hf, csl, tkb = geom(ci, step)
                kr = step % RING
                bo = pr.next_bank()

                def mmo(e):
                    e.matmul(psb(bo)[:, 0:64], Vm[hf][:, b, :], STm[ci][kr][:], start=True, stop=False)
                    return e.matmul(psb(bo)[:, 0:64], SBF[ci][kr][:], QT[:, dr, csl], start=False, stop=True)
                pr.pe(mmo, reads=[f"V{sx}", f"ST{ci}_{kr}", f"SBF{ci}_{kr}", f"QT{sx}"], writes=[f"ps{bo}"])
                pr.dve(lambda e: e.tensor_tensor(out=O[:, csl], in0=psb(bo)[:, 0:64], in1=O[:, csl], op=ALU.add),
                       reads=[f"ps{bo}", "O"], writes=["O"])

            for step in range(nchunk + LAG):
                if step < nchunk:
                    bps = [stage_a(ci, step) for ci in range(len(chains))]
                    for ci in range(len(chains)):
                        stage_b(ci, step, bps[ci])
                if step - LAG >= 0:
                    for ci in range(len(chains)):
                        stage_c(ci, step - LAG)
                if step < nchunk:
                    yield
            if jn == "P":
                for ci, (si, dr) in enumerate(chains):
                    pr.dma("sp", lambda e, ci=ci, si=si, dr=dr: e.dma_start(out=st_out[si, dr, h, :, :], in_=S[ci][:]),
                           key=f"stout{ci}", reads=[f"S{ci}"], out=True)
            for tt in range(ntile):
                ts = slice(tt * 512, (tt + 1) * 512)
                pr.act(lambda e, ts=ts: e.activation(out=sq[:], in_=O[:, ts], func=AF.Square), reads=["O"], writes=["sq"])
                b = pr.next_bank()
                pr.pe(lambda e, b=b: e.matmul(psb(b), ones_b[:], sq[:], start=True, stop=True), reads=["sq", "ones"], writes=[f"ps{b}"])
                pr.act(lambda e, b=b: e.activation(out=rs[:], in_=psb(b), func=AF.Sqrt, bias=sm(SM_EPS, 1), scale=1.0 / 128),
                       reads=[f"ps{b}", "small"], writes=["rstd"])
                pr.dve(lambda e: e.reciprocal(rs[:], rs[:]), reads=["rstd"], writes=["rstd"])
                pr.dve(lambda e, ts=ts: e.tensor_tensor(out=t1[:], in0=O[:, ts], in1=rs[:], op=ALU.mult), reads=["O", "rstd"], writes=["ntmp0"])
                pr.dve(lambda e, ts=ts: e.scalar_tensor_tensor(out=MIX[:, h, ts], in0=t1[:], scalar=sm(SM_GN, 1), in1=GS[:, ts],
                                                               op0=ALU.mult, op1=ALU.mult),
                       reads=["ntmp0", f"GS{sx}", "small"], writes=["MIX"])
                yield

        for ci_ in range(nch):
            for k_ in range(RING):
                pr.dve(lambda e, t_=STm[ci_][k_]: e.memset(t_[:], 0.0), writes=[f"ST{ci_}_{k_}"])
        for sx_ in range(2):
            for vn_ in ("V0", "V1"):
                pr.dve(lambda e, t_=sets[sx_][vn_]: e.memset(t_[:], 0.0), writes=[f"V{sx_}"])
        g0 = prep(0)
        next(g0, None)
        if pre is not None:
            pre()
        for _ in g0:
            pass
        for h in range(A_HEADS):
            sc = scan(h)
            pp = prep(h + 1) if h + 1 < A_HEADS else None
            n_s = job["seqs"][0][1] // 64 + ntile
            n_p = (1 + 2 * ntile + NB + 1) if pp is not None else 0
            done_p = 0
            for i_s in range(n_s):
                next(sc, None)
                if jn == "P":
                    for _ in range(2):
                        if len(mod_todo) > NMC // 2:
                            mod_chunk(*mod_todo.pop(0))
                if pp is not None and i_s >= 1:
                    tgt = (n_p * i_s + n_s - 2) // (n_s - 1) if n_s > 1 else n_p
                    while done_p < min(tgt, n_p):
                        next(pp, None)
                        done_p += 1
            for _ in sc:
                pass
            if pp is not None:
                for _ in pp:
                    pass
        if jn == "P":
            while len(mod_todo) > NMC // 2:
                mod_chunk(*mod_todo.pop(0))
            mod_derived(0, 1)
            mod_derived(1, 0)
        wout0_partial(job, [w0, w0x], MIX, 0, ["wtok", "wfm"])

        pr.barrier()
        pr.phase = f"mixB_{job['name']}"
        o = SCR
        wvs = [sb.at(o + i_ * 16384, [128, DC, 512], BF16) for i_ in range(2)]; o += 32768
        VVN = sb.at(o, [128, NB, 1024], BF16); o += NB * 2048
        MIXB = sb.at(o, [128, 8, nt], BF16); o += nt * 16
        w0b = sb.at(o, [128, 8, 512], BF16); o += 8192
        vnb = sb.at(o, [128, 1024], F32); o += 4096
        bsb = sb.at(o, [128, 4, 512], F32); o += 8192
        wst = sb.at(o, [128, 4, 128], BF16); o += 1024
        gv = sb.at(o, [128, 512], F32); o += 2048
        sqv = sb.at(o, [128, 256], F32); o += 1024
        ssq = sb.at(o, [128, 8], F32); o += 32
        tmpb = sb.at(o, [128, 512], F32); o += 2048
        assert o <= LIM, o
        pr.dma("sp", lambda e: e.dma_start(out=vnb[:], in_=vn_b[:, :]), key="vnb", writes=["vnb"])
        pr.dma("sp", lambda e: e.dma_start(out=bsb[:], in_=bs_b[:, :, :]), key="bsb", writes=["bsb"])
        pr.dma("pool", lambda e: e.dma_start(out=wst[:], in_=wsT[:, :, :]), key="wst", writes=["wst"])
        for half in range(2):
            wv = wvs[half]
            wvn = f"wv{half}"
            pr.dma("pool", lambda e, half=half, wv=wv: e.dma_start(out=wv[:], in_=wvu_r[half]), key=wvn, writes=[wvn])
            for b in range(NB):
                tk = slice(b * 128, (b + 1) * 128)
                tt = b // 4
                bk = pr.next_bank()

                if jn == "P" and mod_todo:
                    mod_chunk(*mod_todo.pop(0))

                def mm(e, tk=tk, bk=bk, wv=wv):
                    ins = None
                    for k in range(DC):
                        ins = e.matmul(psb(bk), hT[:, k, tk], wv[:, k, :], start=(k == 0), stop=(k == DC - 1))
                    return ins
                pr.pe(mm, reads=[wvn, f"hT{tt}"], writes=[f"ps{bk}"])
                pr.act(lambda e, bk=bk: e.activation(out=gv[:], in_=psb(bk), func=AF.Gelu_apprx_tanh), reads=[f"ps{bk}"], writes=["gv"])
                for gi in range(2):
                    pr.act(lambda e, gi=gi: e.activation(out=sqv[:], in_=gv[:, gi * 256:(gi + 1) * 256], func=AF.Square,
                                                         accum_out=ssq[:, gi:gi + 1]),
                           reads=["gv"], writes=["sqv", "ssq"])
                pr.act(lambda e: e.activation(out=ssq[:, 2:4], in_=ssq[:, 0:2], func=AF.Sqrt, bias=sm(SM_EPS, 1), scale=1.0 / 256),
                       reads=["ssq", "small"], writes=["ssq"])
                pr.dve(lambda e: e.reciprocal(ssq[:, 4:6], ssq[:, 2:4]), reads=["ssq"], writes=["ssq"])
                for gi in range(2):
                    c0 = half * 512 + gi * 256
                    pr.dve(lambda e, gi=gi, c0=c0, b=b: e.scalar_tensor_tensor(out=VVN[:, b, c0:c0 + 256], in0=gv[:, gi * 256:(gi + 1) * 256],
                                                                               scalar=ssq[:, 4 + gi:5 + gi], in1=vnb[:, c0:c0 + 256],
                                                                               op0=ALU.mult, op1=ALU.mult),
                           reads=["gv", "ssq", "vnb"], writes=["VVN"])
        for half in range(2):
            wv = wvs[half]
            wvn = f"wv{half}"
            pr.dma("pool", lambda e, half=half, wv=wv: e.dma_start(out=wv[:], in_=wvu_r[2 + half]), key=wvn, writes=[wvn])
            for ccl in range(4):
                cc_ = half * 4 + ccl
                for tt in range(ntile):
                    ts = slice(tt * 512, (tt + 1) * 512)
                    bk = pr.next_bank()

                    if jn == "P" and mod_todo:
                        mod_chunk(*mod_todo.pop(0))

                    def mm(e, ccl=ccl, ts=ts, bk=bk, wv=wv):
                        ins = None
                        for k in range(DC):
                            ins = e.matmul(psb(bk), wv[:, k, ccl * 128:(ccl + 1) * 128], hT[:, k, ts], start=(k == 0), stop=(k == DC - 1))
                        return ins
                    pr.pe(mm, reads=[wvn, f"hT{tt}"], writes=[f"ps{bk}"])
                    pr.act(lambda e, bk=bk, cc_=cc_, ts=ts: e.activation(out=MIXB[:, cc_, ts], in_=psb(bk), func=AF.Gelu_apprx_tanh),
                           reads=[f"ps{bk}"], writes=["MIXB"])
        for cc_ in range(8):
            g = cc_ // 2
            for tt in range(ntile):
                ts = slice(tt * 512, (tt + 1) * 512)
                bk = pr.next_bank()

                if jn == "P" and mod_todo:
                    mod_chunk(*mod_todo.pop(0))

                def mm(e, cc_=cc_, tt=tt, bk=bk, g=g):
                    ins = None
                    for bl in range(4):
                        ins = e.matmul(psb(bk)[:, bl * 128:(bl + 1) * 128], VVN[:, tt * 4 + bl, cc_ * 128:(cc_ + 1) * 128], wst[:, g, :],
                                       start=True, stop=True)
                    return ins
                pr.pe(mm, reads=["VVN", "wst"], writes=[f"ps{bk}"])
                pr.dve(lambda e, bk=bk, g=g: e.tensor_tensor(out=tmpb[:], in0=psb(bk), in1=bsb[:, g, :], op=ALU.add),
                       reads=[f"ps{bk}", "bsb"], writes=["tmpb"])
                pr.dve(lambda e, cc_=cc_, ts=ts: e.tensor_tensor(out=MIXB[:, cc_, ts], in0=tmpb[:], in1=MIXB[:, cc_, ts], op=ALU.mult),
                       reads=["tmpb", "MIXB"], writes=["MIXB"])
        dbg(f"mixB_{jn}", MIXB[:], [128, 8, nt], BF16, ["MIXB"])
        dbg(f"VVN_{jn}", VVN[:], [128, NB, 1024], BF16, ["VVN"])
        if jn == "P":
            while mod_todo:
                mod_chunk(*mod_todo.pop(0))
            mod_derived(1, 1)
        w0bx = sb.at(SCR, [128, 8, 512], BF16)
        wout0_partial(job, [w0b, w0bx], MIXB, 1024, ["w0b", "wv0"], "MIXB")

    def ffn(job, l, pre=None):
        pr.barrier()
        pr.phase = f"ffn{l}_{job['name']}"
        r = job["cond"]
        ntile = job["nt"] // 512
        row = job["row"]
        nrow = 512 // row
        o = SCR
        NWUP = 3
        wup = [sb.at(o + i * 8192, [128, DC, 256], BF16) for i in range(NWUP)]; o += NWUP * 8192
        wds = [sb.at(o + i * 16384, [128, 4, D], BF16) for i in range(2)]; o += 32768
        gbuf = sb.at(o, [128, 4, 1024], BF16); o += 8192
        cg = [sb.at(o + i * 2048, [128, 512], F32) for i in range(2)]; o += 4096
        cv = [sb.at(o + i * 2048, [128, 512], F32) for i in range(2)]; o += 4096
        sg = [sb.at(o + i * 2048, [128, 512], F32) for i in range(2)]; o += 4096
        assert o <= LIM
        cwb = SM_CW[l]

        def cw_ap(j, f):
            return sm(cwb + j * 88 + f, 1)

        def cb_ap(f):
            return sm(cwb + 264 + f, 1)
        it = 0

        def load_wd(grp):
            src = ffn_dn[l][grp * 512:(grp + 1) * 512, :].rearrange("(j p) d -> p j d", p=128)
            wd_ = wds[grp % 2]
            pr.dma("pool", lambda e: e.dma_start(out=wd_[:], in_=src), key=f"wd{grp % 2}", writes=[f"wd{grp % 2}"])

        def load_wup(pair):
            src = ffn_up[l][pair]
            s_ = pair % NWUP
            pr.dma("pool", lambda e: e.dma_start(out=wup[s_][:], in_=src), key=f"wup{s_}", writes=[f"wup{s_}"])

        load_wup(0)
        load_wd(0)
        load_wup(1)
        load_wup(2)
        if pre is not None:
            pre()
        for grp in range(NPAIR // 4):
            wd = wds[grp % 2]
            wdn = f"wd{grp % 2}"
            if grp > 0:
                load_wd(grp)
            for j in range(4):
                pair = grp * 4 + j
                slot = pair % NWUP
                if pair > 2:
                    load_wup(pair)
                for tt in range(ntile):
                    ts = slice(tt * 512, (tt + 1) * 512)
                    bg = pr.next_bank()
                    bv = pr.next_bank()

                    def mm(e, s=slot, ts=ts, bg=bg, bv=bv):
                        ins = None
                        for k in range(DC):
                            ins = e.matmul(psb(bg), wup[s][:, k, 0:128], hT[:, k, ts], start=(k == 0), stop=(k == DC - 1))
                        for k in range(DC):
                            ins = e.matmul(psb(bv), wup[s][:, k, 128:256], hT[:, k, ts], start=(k == 0), stop=(k == DC - 1))
                        return ins
                    pr.pe(mm, reads=[f"wup{slot}", f"hT{tt}"], writes=[f"ps{bg}", f"ps{bv}"])
                    q = it % 2
                    it += 1
                    fg, fv = pair, NPAIR + pair
                    pr.act(lambda e, q=q, bg=bg, fg=fg: e.activation(out=cg[q][:], in_=psb(bg), func=AF.Identity,
                                                                     bias=cb_ap(fg), scale=cw_ap(1, fg)),
                           reads=[f"ps{bg}", "small"], writes=[f"cg{q}"])
                    pr.act(lambda e, q=q, bv=bv, fv=fv: e.activation(out=cv[q][:], in_=psb(bv), func=AF.Identity,
                                                                     bias=cb_ap(fv), scale=cw_ap(1, fv)),
                           reads=[f"ps{bv}", "small"], writes=[f"cv{q}"])
                    for (buf, bn, bb, ff) in ((cg, "cg", bg, fg), (cv, "cv", bv, fv)):
                        def v3(ap):
                            return ap.rearrange("p (a b) -> p a b", a=nrow)
                        pr.dve(lambda e, buf=buf, bb=bb, ff=ff, q=q, v3=v3: e.scalar_tensor_tensor(
                            out=v3(buf[q][:])[:, :, 1:], in0=v3(psb(bb))[:, :, :row - 1], scalar=cw_ap(0, ff),
                            in1=v3(buf[q][:])[:, :, 1:], op0=ALU.mult, op1=ALU.add),
                            reads=[f"ps{bb}", f"{bn}{q}", "small"], writes=[f"{bn}{q}"])
                    for (buf, bn, bb, ff) in ((cg, "cg", bg, fg), (cv, "cv", bv, fv)):
                        def v3(ap):
                            return ap.rearrange("p (a b) -> p a b", a=nrow)
                        pr.dve(lambda e, buf=buf, bb=bb, ff=ff, q=q, v3=v3: e.scalar_tensor_tensor(
                            out=v3(buf[q][:])[:, :, :row - 1], in0=v3(psb(bb))[:, :, 1:], scalar=cw_ap(2, ff),
                            in1=v3(buf[q][:])[:, :, :row - 1], op0=ALU.mult, op1=ALU.add),
                            reads=[f"ps{bb}", f"{bn}{q}", "small"], writes=[f"{bn}{q}"])
                    pr.act(lambda e, q=q: e.activation(out=sg[q][:], in_=cg[q][:], func=AF.Silu),
                           reads=[f"cg{q}"], writes=[f"sg{q}"])
                    pr.dve(lambda e, q=q, j=j, ts=ts: e.tensor_tensor(out=gbuf[:, j, ts], in0=sg[q][:], in1=cv[q][:],
                                                                      op=ALU.mult),
                           reads=[f"sg{q}", f"cv{q}"], writes=["gbuf"])
            for dc in range(DC):
                for tt in range(ntile):
                    ts = slice(tt * 512, (tt + 1) * 512)
                    b = pr.next_bank()

                    def mm(e, dc=dc, ts=ts, b=b, wd=wd):
                        ins = None
                        for j in range(4):
                            ins = e.matmul(psb(b), wd[:, j, dc * 128:(dc + 1) * 128], gbuf[:, j, ts],
                                           start=(j == 0), stop=(j == 3))
                        return ins
                    pr.pe(mm, reads=[wdn, "gbuf"], writes=[f"ps{b}"])
                    accum_x(job, b, dc, tt, mod_ap(l, 5, r, dc))

    def fnet(job, l=1, pre=None):
        pr.barrier()
        pr.phase = f"fnet_{job['name']}"
        r = job["cond"]
        nt = job["nt"]
        ntile = nt // 512
        o = SCR
        cc = sb.at(o, [128, 4, 2, 512], BF16); o += 8192
        L = job["seqs"][0][1]
        nbs = L // 128
        dn = sb.at(o, [128, nbs, 2, L], BF16); o += nbs * 2 * L * 2
        Y = sb.at(o, [128, 8, 2, 512], BF16); o += 16384
        Z = sb.at(o, [128, 4, 1024], BF16); o += 8192
        w1 = sb.at(o, [128, 4, D], BF16); o += 16384
        assert o <= LIM, o
        pr.dma("pool", lambda e: e.dma_start(out=cc[:], in_=dftc.rearrange("(k p) s f -> p k s f", p=128)),
               key="dftc", writes=["dftc"])
        dsrc = dft256 if L == 256 else dft1024
        pr.dma("pool", lambda e: e.dma_start(out=dn[:], in_=dsrc.rearrange("(k p) s f -> p k s f", p=128)),
               key="dftn", writes=["dftn"])
        ev = 0

        def load_w1(g):
            src = w_out1[g * 512:(g + 1) * 512, :].rearrange("(j p) d -> p j d", p=128)
            pr.dma("pool", lambda e: e.dma_start(out=w1[:], in_=src), key="w1", writes=["w1"])

        load_w1(0)
        if pre is not None:
            pre()
        for g in range(4):
            if g > 0:
                load_w1(g)
            for (s0, L_) in job["seqs"]:
                for pb in range(nbs):
                    tk = slice(s0 + pb * 128, s0 + (pb + 1) * 128)
                    tt = (s0 + pb * 128) // 512
                    for cs in range(2):
                        b = pr.next_bank()

                        def mm(e, g=g, tk=tk, cs=cs, b=b):
                            ins = None
                            for kc in range(4):
                                ins = e.matmul(psb(b), hT[:, g * 4 + kc, tk], cc[:, kc, cs, :], start=(kc == 0), stop=(kc == 3))
                            return ins
                        pr.pe(mm, reads=[f"hT{tt}", "dftc"], writes=[f"ps{b}"])
                        if ev % 2 == 0:
                            pr.act(lambda e, pb=pb, cs=cs, b=b: e.activation(out=Y[:, pb, cs, :], in_=psb(b), func=AF.Copy),
                                   reads=[f"ps{b}"], writes=["Y"])
                        else:
                            pr.dve(lambda e, pb=pb, cs=cs, b=b: e.tensor_copy(Y[:, pb, cs, :], psb(b)),
                                   reads=[f"ps{b}"], writes=["Y"])
                        ev += 1
                pw = min(L_, 512)
                for cq in range(4):
                    for pt in range(L_ // pw):
                        b = pr.next_bank()

                        def mm(e, cq=cq, pt=pt, b=b, pw=pw):
                            ins = None
                            n = 0
                            for kb in range(nbs):
                                for cs in range(2):
                                    ins = e.matmul(psb(b, pw), Y[:, kb, cs, cq * 128:(cq + 1) * 128],
                                                   dn[:, kb, cs, pt * pw:(pt + 1) * pw],
                                                   start=(n == 0), stop=(n == 2 * nbs - 1))
                                    n += 1
                            return ins
                        pr.pe(mm, reads=["Y", "dftn"], writes=[f"ps{b}"])
                        zs = slice(s0 + pt * pw, s0 + (pt + 1) * pw)
                        if ev % 2 == 0:
                            pr.act(lambda e, cq=cq, zs=zs, b=b, pw=pw: e.activation(out=Z[:, cq, zs], in_=psb(b, pw), func=AF.Copy),
                                   reads=[f"ps{b}"], writes=["Z"])
                        else:
                            pr.dve(lambda e, cq=cq, zs=zs, b=b, pw=pw: e.tensor_copy(Z[:, cq, zs], psb(b, pw)),
                                   reads=[f"ps{b}"], writes=["Z"])
                        ev += 1
            for dc in range(DC):
                for tt in range(ntile):
                    ts = slice(tt * 512, (tt + 1) * 512)
                    b = pr.next_bank()

                    def mm(e, dc=dc, ts=ts, b=b):
                        ins = None
                        for j in range(4):
                            ins = e.matmul(psb(b), w1[:, j, dc * 128:(dc + 1) * 128], Z[:, j, ts], start=(j == 0), stop=(j == 3))
                        return ins
                    pr.pe(mm, reads=["w1", "Z"], writes=[f"ps{b}"])
                    accum_x(job, b, dc, tt, mod_ap(l, 2, r, dc))

    def final(job):
        pr.barrier()
        pr.phase = f"final_{job['name']}"
        ys = sb.at(SCR, [128, DC, 512], F32)
        for tt in range(job["nt"] // 512):
            norm_tile(job, tt, scale_fn=lambda c: sm(SM_GAIN + 4 * 16 + c, 1), bias_fn=lambda c: None,
                      out_fn=lambda c: ys[:, c, :], out_res="ys")
            t0 = job["t0"] + tt * 512
            dst = yT_d[:, t0:t0 + 512].rearrange("(c p) t -> p c t", p=128)
            pr.dma("sp", lambda e, dst=dst: e.dma_start(out=dst, in_=ys[:]), key="yout", reads=["ys"], out=True)

    for job in JOBS:
        nt = job["nt"]
        src = xT_d[:, job["t0"]:job["t0"] + nt].rearrange("(c p) t -> p c t", p=128)
        pr.dma("sp", lambda e, src=src, nt=nt: e.dma_start(out=xT[:, :, 0:nt], in_=src), key="xin",
               writes=[f"xT{tt}" for tt in range(nt // 512)])
        if STAGE >= 3:
            mixer0(job, pre=lambda: adaln(job, 0, 0))
        ffn(job, 0, pre=lambda: adaln(job, 0, 1))
        if STAGE >= 2:
            fnet(job, pre=lambda: adaln(job, 1, 0))
        ffn(job, 1, pre=lambda: adaln(job, 1, 1))
        final(job)

    pr.emit(nc)
    nc._dbg_names = dbg_names
    return nc


def fm(v):
    v = np.asarray(v, np.float32)
    return np.ascontiguousarray(v.reshape(-1, 128).T)


def make_consts():
    c = np.zeros((128, 1024), np.float32)
    c[:, 0:128] = np.eye(128, dtype=np.float32)
    j = np.arange(128)[:, None]
    i = np.arange(128)[None, :]
    same = (j // 64 == i // 64).astype(np.float32)
    jl = j % 64
    cmf = np.zeros((128, 134), np.float32)
    cmb = np.zeros((128, 134), np.float32)
    cmf[:, :128] = same * ((j <= i).astype(np.float32) - (jl <= 31).astype(np.float32))
    cmb[:, :128] = same * ((j >= i).astype(np.float32) - (jl >= 32).astype(np.float32))
    for ch in range(2):
        inch = (j[:, 0] // 64 == ch).astype(np.float32)
        cmf[:, 128 + 3 * ch + 0] = inch * (jl[:, 0] <= 31)
        cmf[:, 128 + 3 * ch + 1] = inch
        cmf[:, 128 + 3 * ch + 2] = inch * (jl[:, 0] > 31)
        cmb[:, 128 + 3 * ch + 0] = inch * (jl[:, 0] >= 32)
        cmb[:, 128 + 3 * ch + 1] = inch
        cmb[:, 128 + 3 * ch + 2] = inch * (jl[:, 0] < 32)
    c[:, 128:262] = cmf
    c[:, 262:396] = cmb
    c[:, 396:524] = -cmf[:, :128]
    c[:, 524:652] = -cmb[:, :128]
    ii = np.arange(64)[None, :]
    for hf in range(2):
        inh = (j // 64 == hf).astype(np.float32)
        c[:, 652 + (hf * 2 + 0) * 64:652 + (hf * 2 + 1) * 64] = inh * (jl <= ii)
        c[:, 652 + (hf * 2 + 1) * 64:652 + (hf * 2 + 2) * 64] = inh * (jl >= ii)
    return c


def dft_mats(n):
    k = np.arange(n, dtype=np.float64)
    ang = 2.0 * np.pi * np.outer(k, k) / n
    s = 1.0 / np.sqrt(n)
    return (np.cos(ang) * s), (np.sin(ang) * s)


_NC_CACHE = {}


def kernel(**inp):
    f32 = np.float32
    inp = {k: np.asarray(v) for k, v in inp.items()}
    if "nc" not in _NC_CACHE:
        _NC_CACHE["nc"] = build_program()
    nc = _NC_CACHE["nc"]

    small = np.zeros((128, 1024), f32)
    small[:, 0:96] = fm(inp["mod_b_0"])
    small[:, 96:192] = fm(inp["mod_b_1"])
    for i, nm in enumerate(["norm1_0", "norm2_0", "norm1_1", "norm2_1", "final_norm"]):
        small[:, 192 + 16 * i:192 + 16 * (i + 1)] = fm(inp[nm])
    for l in range(2):
        base = 272 + 352 * l
        cw = inp[f"ffn_conv_w_{l}"]
        for j in range(3):
            small[:, base + j * 88:base + (j + 1) * 88] = fm(cw[j])
        small[:, base + 264:base + 352] = fm(inp[f"ffn_conv_b_{l}"])
    small[:, 976] = inp["hgrn_gnorm_0"]
    small[:, 977] = EPS
    small[:, 978] = 1.0
    cst = make_consts()
    cc, sc = dft_mats(512)
    dftc = np.stack([cc, sc], axis=1).astype(f32)
    c2, s2 = dft_mats(256)
    dft256 = np.stack([c2, -s2], axis=1).astype(f32)
    c1, s1 = dft_mats(1024)
    dft1024 = np.stack([c1, -s1], axis=1).astype(f32)
    lb_b = np.ascontiguousarray(np.broadcast_to(inp["hgrn_lb"].reshape(2, 2, A_HEADS, 128).transpose(2, 0, 1, 3)[:, None], (A_HEADS, 128, 2, 2, 128))).astype(f32)
    vn_b = np.ascontiguousarray(np.broadcast_to(inp["gmlp_vnorm_0"][None], (128, 1024))).astype(f32)
    bs_b = np.ascontiguousarray(np.broadcast_to(inp["gmlp_bs_0"][None, :, None, :], (128, 4, 4, 128))).reshape(128, 4, 512).astype(f32)
    wsT = np.ascontiguousarray(inp["gmlp_ws_0"].transpose(2, 0, 1)).astype(f32)
    W = inp["w_in_0"].reshape(DC, 128, 7, A_HEADS, 128)
    wtok_r = np.ascontiguousarray(W[:, :, [1, 2, 3]].transpose(3, 1, 0, 2, 4)).reshape(A_HEADS, 128, DC, 384)
    wfm_r = np.ascontiguousarray(W[:, :, [0, 4]].transpose(3, 1, 0, 2, 4)).reshape(A_HEADS, 128, DC, 256)
    Wb = inp["w_in_0"].reshape(DC, 128, 14, 512)
    wvu_r = np.ascontiguousarray(Wb[:, :, [12, 13, 10, 11]].transpose(2, 1, 0, 3))
    shared = dict(small=small, cst=cst, dftc=dftc, dft256=dft256, dft1024=dft1024, lb_b=lb_b, vn_b=vn_b,
                  bs_b=bs_b, wsT=wsT, wtok_r=wtok_r, wfm_r=wfm_r, wvu_r=wvu_r,
                  w_out_0=inp["w_out_0"], w_out_1=inp["w_out_1"])
    for l in range(2):
        shared[f"mod_w_r_{l}"] = np.ascontiguousarray(inp[f"mod_w_{l}"].reshape(DC, 128, 6 * D // 256, 256).transpose(2, 1, 0, 3))
        U = inp[f"ffn_up_{l}"].reshape(DC, 128, 2, NPAIR, 128)
        shared[f"ffn_up_r_{l}"] = np.ascontiguousarray(U.transpose(3, 1, 0, 2, 4)).reshape(NPAIR, 128, DC, 256)
        shared[f"ffn_down_{l}"] = inp[f"ffn_down_{l}"]

    in_maps = []
    for c in range(NCORES):
        xs = np.concatenate([inp["x_prompt"][2 * c], inp["x_prompt"][2 * c + 1], inp["x_sample"][c]], axis=0)
        m = dict(shared)
        m["xT"] = np.ascontiguousarray(xs.T)
        m["st_in"] = np.ascontiguousarray(inp["state_l0_hgrn"][c])
        cond = np.stack([inp["c_ctx"], inp["c"][c]], axis=1)
        m["condT"] = np.ascontiguousarray(cond.reshape(DC, 128, 2).transpose(1, 0, 2))
        in_maps.append(m)

    res = run_bass_kernel_spmd(nc, in_maps, core_ids=list(range(NCORES)))
    if getattr(nc, "_dbg_names", None):
        _NC_CACHE["dbg"] = {k: np.asarray(res.results[0][k]).astype(np.float32) for k in nc._dbg_names}
    y_prompt = np.zeros((16, 256, D), f32)
    y_sample = np.zeros((8, 1024, D), f32)
    st_new = np.zeros((16, 2, A_HEADS, 128, 128), f32)
    for c in range(NCORES):
        y = res.results[c]["yT"].T
        y_prompt[2 * c] = y[0:256]
        y_prompt[2 * c + 1] = y[256:512]
        y_sample[c] = y[512:1536]
        st_new[2 * c:2 * c + 2] = res.results[c]["st_out"]
    return y_prompt, y_sample, st_new
```

```python
import contextlib
import os
import numpy as np
import concourse.bass as bass
import concourse.mybir as mybir
from concourse.bass_utils import run_bass_kernel_spmd

F32 = mybir.dt.float32
BF16 = mybir.dt.bfloat16
AF = mybir.ActivationFunctionType
ALU = mybir.AluOpType
AX = mybir.AxisListType

NCORES = 8
D = 2048
DC = 16
DFF = 5632
NPAIR = 44
NTOK = 1536
EPS = 1e-6
A_HEADS = 8
JOBS = [
    dict(name="P", t0=0, nt=512, seqs=[(0, 256), (256, 256)], row=256, cond=0),
    dict(name="S", t0=512, nt=1024, seqs=[(0, 1024)], row=64, cond=1),
]
STAGE = int(os.environ.get("MK_STAGE", "99"))


class Op:
    __slots__ = ("id", "eng", "fn", "deps", "dmakey", "ms", "semval", "phase")


class Prog:
    ENGS = ("pe", "act", "dve", "pool", "sp")
    CH = 4000

    def __init__(self):
        self.ops = []
        self.lastw = {}
        self.rd_eng = {}
        self.rd_dma = {}
        self.bank = 0
        self.out_keys = set()
        self.bar = set()
        self.phase = "init"
        self.scopes = bool(int(os.environ.get("MK_SCOPES", "0")))

    def add(self, eng, fn, reads=(), writes=(), dmakey=None):
        op = Op()
        op.id = len(self.ops)
        op.eng = eng
        op.fn = fn
        op.dmakey = dmakey
        op.ms = None
        op.semval = None
        op.phase = self.phase
        deps = set()
        for r in reads:
            w = self.lastw.get(r)
            if w is not None:
                deps.add(w)
        for w_ in writes:
            w = self.lastw.get(w_)
            if w is not None:
                deps.add(w)
            for v in self.rd_eng.get(w_, {}).values():
                deps.add(v)
            for v in self.rd_dma.get(w_, ()):
                deps.add(v)
        deps |= self.bar
        op.deps = deps
        for r in reads:
            if dmakey is not None:
                self.rd_dma.setdefault(r, []).append(op.id)
            else:
                self.rd_eng.setdefault(r, {})[eng] = op.id
        for w_ in writes:
            self.lastw[w_] = op.id
            self.rd_eng[w_] = {}
            self.rd_dma[w_] = []
        self.ops.append(op)
        return op

    def pe(self, fn, reads=(), writes=()):
        return self.add("pe", fn, reads, writes)

    def act(self, fn, reads=(), writes=()):
        return self.add("act", fn, reads, writes)

    def dve(self, fn, reads=(), writes=()):
        return self.add("dve", fn, reads, writes)

    def dma(self, q, fn, key, reads=(), writes=(), out=False):
        if out:
            self.out_keys.add(key)
        return self.add(q, fn, reads, writes, dmakey=key)

    def barrier(self):
        last = {}
        for op in self.ops:
            last[op.eng if op.dmakey is None else ("d", op.dmakey)] = op.id
        self.bar = set(last.values())

    def next_bank(self):
        b = self.bank
        self.bank = (self.bank + 1) % 8
        return b

    def emit(self, nc):
        ops = self.ops
        by_eng = {e: [] for e in self.ENGS}
        for op in ops:
            by_eng[op.eng].append(op)
        needed = set()
        for op in ops:
            for d in op.deps:
                dop = ops[d]
                if dop.dmakey is None and not (dop.eng == "pe" and op.eng == "pe"):
                    needed.add(d)
        nsem = {}
        for e in ("pe", "act", "dve"):
            cnt = 0
            for op in by_eng[e]:
                if op.id in needed:
                    op.ms = cnt
                    cnt += 1
            nsem[e] = (cnt + self.CH - 1) // self.CH
        keycnt = {}
        for op in ops:
            if op.dmakey is not None:
                keycnt[op.dmakey] = keycnt.get(op.dmakey, 0) + 1
                op.semval = 16 * keycnt[op.dmakey]
        with contextlib.ExitStack() as es:
            engsem = {e: [es.enter_context(nc.semaphore(f"s_{e}{i}")) for i in range(max(1, nsem[e]))]
                      for e in ("pe", "act", "dve")}
            dmasem = {k: es.enter_context(nc.semaphore(f"d_{k}")) for k in keycnt}
            block = es.enter_context(nc.Block())

            def body(h, e):
                waited = {}
                for op in by_eng[e]:
                    for d in sorted(op.deps):
                        dop = ops[d]
                        if dop.dmakey is not None:
                            sk = ("d", dop.dmakey)
                            sem = dmasem[dop.dmakey]
                            val = dop.semval
                        else:
                            if dop.eng == "pe" and e == "pe":
                                continue
                            ci = dop.ms // self.CH
                            sk = (dop.eng, ci)
                            sem = engsem[dop.eng][ci]
                            val = dop.ms % self.CH + 1
                        if waited.get(sk, 0) >= val:
                            continue
                        h.wait_ge(sem, val)
                        waited[sk] = val
                    if self.scopes:
                        with nc.named_scope(op.phase):
                            ins = op.fn(h)
                    else:
                        ins = op.fn(h)
                    if op.dmakey is not None:
                        ins.then_inc(dmasem[op.dmakey], 16)
                    elif op.id in needed:
                        ins.then_inc(engsem[e][op.ms // self.CH], 1)
                if e == "sp":
                    for k in sorted(self.out_keys):
                        h.wait_ge(dmasem[k], 16 * keycnt[k])

            @block.tensor
            def _(t):
                body(t, "pe")

            @block.scalar
            def _(a):
                body(a, "act")

            @block.vector
            def _(v):
                body(v, "dve")

            @block.gpsimd
            def _(g):
                body(g, "pool")

            @block.sync
            def _(s):
                body(s, "sp")


class SB:
    def __init__(self, nc):
        self.nc = nc
        self.n = 0

    def at(self, off, shape, dt):
        self.n += 1
        return self.nc.alloc_sbuf_tensor_at(f"t{self.n}", shape, dt, offset=off + 16512)


def build_program():
    nc = bass.Bass("TRN2", target_bir_lowering=False)
    pr = Prog()

    def din(name, shape, dt=F32):
        return nc.dram_tensor(name, list(shape), dt, kind="ExternalInput").ap()

    def dout(name, shape, dt=F32):
        return nc.dram_tensor(name, list(shape), dt, kind="ExternalOutput").ap()

    xT_d = din("xT", [D, NTOK])
    yT_d = dout("yT", [D, NTOK])
    st_in = din("st_in", [2, A_HEADS, 128, 128])
    st_out = dout("st_out", [2, 2, A_HEADS, 128, 128])
    condT = din("condT", [128, DC, 2])
    small = din("small", [128, 1024])
    modw = [din(f"mod_w_r_{l}", [6 * D // 256, 128, DC, 256]) for l in range(2)]
    wtok_r = din("wtok_r", [A_HEADS, 128, DC, 384])
    wfm_r = din("wfm_r", [A_HEADS, 128, DC, 256])
    wvu_r = din("wvu_r", [4, 128, DC, 512])
    w_out0 = din("w_out_0", [D, D])
    w_out1 = din("w_out_1", [D, D])
    ffn_up = [din(f"ffn_up_r_{l}", [NPAIR, 128, DC, 256]) for l in range(2)]
    ffn_dn = [din(f"ffn_down_{l}", [DFF, D]) for l in range(2)]
    lb_b = din("lb_b", [A_HEADS, 128, 2, 2, 128])
    vn_b = din("vn_b", [128, 1024])
    bs_b = din("bs_b", [128, 4, 512])
    wsT = din("wsT", [128, 4, 128])
    cst = din("cst", [128, 1024])
    dftc = din("dftc", [512, 2, 512])
    dft256 = din("dft256", [256, 2, 256])
    dft1024 = din("dft1024", [1024, 2, 1024])

    DBG = bool(int(os.environ.get("MK_DEBUG", "0")))
    dbg_names = []

    def dbg(name, ap, shape, dt, reads):
        if not DBG:
            return
        d = dout("dbg_" + name, shape, dt)
        dbg_names.append("dbg_" + name)
        pr.dma("sp", lambda e: e.dma_start(out=d, in_=ap), key="dbg_" + name, reads=reads, out=True)

    sb = SB(nc)
    xT = sb.at(0, [128, DC, 1024], F32)
    hT = sb.at(65536, [128, DC, 1024], BF16)
    o = 98304
    smallt = sb.at(o, [128, 1024], F32); o += 4096
    modT = [sb.at(o + l * 768, [128, 96, 2], F32) for l in range(2)]; o += 1536
    aT = sb.at(o, [128, 2, 2, 2, DC], F32); o += 512
    cond32 = sb.at(o, [128, DC, 2], F32); o += 128
    scT = sb.at(o, [128, DC, 2], BF16); o += 64
    ones_b = sb.at(o, [128, 128], BF16); o += 256
    ident_b = sb.at(o, [128, 128], BF16); o += 256
    rstd = sb.at(o, [128, 512], F32); o += 2048
    ntmp = [sb.at(o + i * 2048, [128, 512], F32) for i in range(2)]; o += 4096
    cstt = sb.at(o, [128, 1024], F32); o += 4096
    SCR = o
    assert SCR <= 120000, SCR
    LIM = 212800

    ps_all = nc.alloc_psum_tensor("ps_all", [128, 8 * 512], F32)

    def psb(b, n=512):
        return ps_all[:, b * 512:b * 512 + n]

    def sm(off, n):
        return smallt[:, off:off + n]

    SM_MODB = [0, 96]
    SM_GAIN = 192
    SM_CW = [272, 272 + 352]
    SM_GN = 976
    SM_EPS = 977
    SM_ONE = 978

    pr.dma("sp", lambda e: e.dma_start(out=smallt[:], in_=small[:, :]), key="small", writes=["small"])
    pr.dma("sp", lambda e: e.dma_start(out=cond32[:], in_=condT[:, :, :]), key="cond", writes=["cond32"])
    pr.dma("sp", lambda e: e.dma_start(out=cstt[:], in_=cst[:, :]), key="cst", writes=["cst"])
    pr.dve(lambda e: e.memset(ones_b[:], 1.0), writes=["ones"])
    pr.dve(lambda e: e.tensor_copy(ident_b[:], cstt[:, 0:128]), reads=["cst"], writes=["ident"])
    pr.act(lambda e: e.activation(out=scT[:], in_=cond32[:], func=AF.Silu), reads=["cond32"], writes=["scT"])

    pr.phase = "mod"
    CW = 256
    NMC = 6 * D // CW
    MWOFF = 212800 - 2 * 8192
    mw = [sb.at(MWOFF + i * 8192, [128, DC, CW], BF16) for i in range(2)]
    mod_state = {"n": 0}

    def mod_chunk(l, ci):
        slot = mod_state["n"] % 2
        mod_state["n"] += 1
        pr.dma("pool", lambda e: e.dma_start(out=mw[slot][:], in_=modw[l][ci]), key=f"mw{slot}", writes=[f"mw{slot}"])
        b = pr.next_bank()
        nb = CW // 128

        def mm(e):
            ins = None
            for j in range(nb):
                for k in range(DC):
                    ins = e.matmul(psb(b)[:, 2 * j:2 * j + 2], mw[slot][:, k, j * 128:(j + 1) * 128],
                                   scT[:, k, :], start=(k == 0), stop=(k == DC - 1))
            return ins
        pr.pe(mm, reads=[f"mw{slot}", "scT"], writes=[f"ps{b}"])
        c0 = ci * nb
        for r in range(2):
            pr.dve(lambda e, r=r: e.tensor_tensor(out=modT[l][:, c0:c0 + nb, r], in0=psb(b)[:, r:2 * nb:2],
                                                  in1=sm(SM_MODB[l] + c0, nb), op=ALU.add),
                   reads=[f"ps{b}", "small"], writes=[f"modT{l}"])

    def mod_derived(l, wh):
        for r in range(2):
            sc_ap = modT[l][:, (1 + 3 * wh) * 16:(2 + 3 * wh) * 16, r]
            g_ap = sm(SM_GAIN + (2 * l + wh) * 16, 16)
            pr.dve(lambda e, sc_ap=sc_ap, g_ap=g_ap, r=r: e.scalar_tensor_tensor(
                out=aT[:, l, wh, r, :], in0=sc_ap, scalar=1.0, in1=g_ap, op0=ALU.add, op1=ALU.mult),
                reads=[f"modT{l}", "small"], writes=["aT"])

    for ci in range(2 * D // CW):
        mod_chunk(0, ci)
    mod_derived(0, 0)
    mod_todo = [(0, ci) for ci in range(2 * D // CW, NMC)] + [(1, ci) for ci in range(NMC)]

    def mod_ap(l, idx, r, c):
        return modT[l][:, idx * 16 + c, r:r + 1]

    def norm_tile(job, tt, scale_fn, bias_fn, out_fn, out_res, out_dt_is_f32=False):
        ts = slice(tt * 512, (tt + 1) * 512)
        pr.act(lambda e: e.activation(out=hT[:, :, ts], in_=xT[:, :, ts], func=AF.Square),
               reads=[f"xT{tt}"], writes=[f"hT{tt}"])
        b = pr.next_bank()

        def mm(e):
            ins = None
            for k in range(DC):
                ins = e.matmul(psb(b), ones_b[:], hT[:, k, ts], start=(k == 0), stop=(k == DC - 1))
            return ins
        pr.pe(mm, reads=[f"hT{tt}", "ones"], writes=[f"ps{b}"])
        pr.act(lambda e: e.activation(out=rstd[:], in_=psb(b), func=AF.Sqrt, bias=sm(SM_EPS, 1), scale=1.0 / D),
               reads=[f"ps{b}", "small"], writes=["rstd"])
        pr.dve(lambda e: e.reciprocal(rstd[:], rstd[:]), reads=["rstd"], writes=["rstd"])
        for c in range(DC):
            t = ntmp[c % 2]
            tn = f"ntmp{c % 2}"
            pr.dve(lambda e, c=c, t=t: e.tensor_tensor(out=t[:], in0=xT[:, c, ts], in1=rstd[:], op=ALU.mult),
                   reads=[f"xT{tt}", "rstd"], writes=[tn])
            bi = bias_fn(c)
            if bi is None:
                pr.act(lambda e, c=c, t=t: e.activation(out=out_fn(c), in_=t[:], func=AF.Copy, scale=scale_fn(c)),
                       reads=[tn, "small", "aT", "modT0", "modT1"], writes=[out_res])
            else:
                pr.act(lambda e, c=c, t=t, bi=bi: e.activation(out=out_fn(c), in_=t[:], func=AF.Identity,
                                                               bias=bi, scale=scale_fn(c)),
                       reads=[tn, "small", "aT", "modT0", "modT1"], writes=[out_res])

    def adaln(job, l, wh):
        r = job["cond"]
        for tt in range(job["nt"] // 512):
            ts = slice(tt * 512, (tt + 1) * 512)
            norm_tile(job, tt,
                      scale_fn=lambda c: aT[:, l, wh, r, c:c + 1],
                      bias_fn=lambda c: mod_ap(l, 3 * wh, r, c),
                      out_fn=lambda c, ts=ts: hT[:, c, ts], out_res=f"hT{tt}")

    def accum_x(job, b, dc, tt, gate_ap):
        ts = slice(tt * 512, (tt + 1) * 512)
        pr.dve(lambda e: e.scalar_tensor_tensor(out=xT[:, dc, ts], in0=psb(b), scalar=gate_ap, in1=xT[:, dc, ts],
                                                op0=ALU.mult, op1=ALU.add),
               reads=[f"ps{b}", f"xT{tt}", "modT0", "modT1"], writes=[f"xT{tt}"])


    def gelu_tanh(src_ps, n, dst, xs, wk, res_ps, res_dst):
        pr.act(lambda e: e.activation(out=xs, in_=src_ps, func=AF.Copy), reads=[res_ps], writes=["g_xs"])
        pr.dve(lambda e: e.tensor_tensor(out=wk, in0=xs, in1=xs, op=ALU.mult), reads=["g_xs"], writes=["g_wk"])
        pr.dve(lambda e: e.tensor_scalar(out=wk, in0=wk, scalar1=0.044715, scalar2=1.0, op0=ALU.mult, op1=ALU.add),
               reads=["g_wk"], writes=["g_wk"])
        pr.dve(lambda e: e.tensor_tensor(out=wk, in0=wk, in1=xs, op=ALU.mult), reads=["g_wk", "g_xs"], writes=["g_wk"])
        pr.act(lambda e: e.activation(out=wk, in_=wk, func=AF.Sigmoid, scale=1.5957691216057308),
               reads=["g_wk"], writes=["g_wk"])
        pr.dve(lambda e: e.tensor_tensor(out=dst, in0=wk, in1=xs, op=ALU.mult), reads=["g_wk", "g_xs"], writes=[res_dst])

    def wout0_partial(job, w0s, MIX, row0, w0ress, mixres="MIX"):
        r = job["cond"]
        ntile = job["nt"] // 512
        for dcg in range(4):
            src = w_out0[row0:row0 + 1024, dcg * 512:(dcg + 1) * 512].rearrange("(k p) f -> p k f", p=128)
            w0 = w0s[dcg % len(w0s)]
            w0res = w0ress[dcg % len(w0s)]
            pr.dma("pool", lambda e, src=src, w0=w0: e.dma_start(out=w0[:], in_=src), key=w0res, writes=[w0res])
            for dcl in range(4):
                dc = dcg * 4 + dcl
                for tt in range(ntile):
                    ts = slice(tt * 512, (tt + 1) * 512)
                    b = pr.next_bank()

                    def mm(e, dcl=dcl, ts=ts, b=b, w0=w0):
                        ins = None
                        for k in range(8):
                            ins = e.matmul(psb(b), w0[:, k, dcl * 128:(dcl + 1) * 128], MIX[:, k, ts], start=(k == 0), stop=(k == 7))
                        return ins
                    pr.pe(mm, reads=[w0res, mixres], writes=[f"ps{b}"])
                    accum_x(job, b, dc, tt, mod_ap(0, 2, r, dc))

    def mixer0(job, pre=None):
        pr.barrier()
        pr.phase = f"mixA_{job['name']}"
        r = job["cond"]
        nt = job["nt"]
        ntile = nt // 512
        NB = nt // 128
        jn = job["name"]
        nch = 2 * len(job["seqs"])
        o = SCR
        wtok = sb.at(o, [128, DC, 384], BF16)
        w0 = sb.at(o, [128, 8, 512], BF16); o += 12288
        wfm = sb.at(o, [128, DC, 256], BF16)
        w0x = sb.at(o, [128, 8, 512], BF16); o += 8192
        sets = []
        for i_ in range(2):
            X = {}
            X["KT"] = sb.at(o, [128, NB, 256], BF16); o += NB * 512
            X["KTT"] = sb.at(o, [128, 2, nt], BF16); o += nt * 4
            X["QT"] = sb.at(o, [128, 2, nt], BF16); o += nt * 4
            X["V0"] = sb.at(o, [128, NB, 128], BF16); o += NB * 256
            X["V1"] = sb.at(o, [128, NB, 128], BF16); o += NB * 256
            X["SCAL"] = sb.at(o, [128, NB, 2, 6], F32); o += NB * 48
            X["GS"] = sb.at(o, [128, nt], BF16); o += nt * 2
            X["oml"] = sb.at(o, [128, 2, 128], F32); o += 1024
            sets.append(X)
        lbt = sb.at(o, [128, 2, 2, 128], F32); o += 2048
        QS = sb.at(o, [128, nt], BF16); o += nt * 2
        LFr = sb.at(o, [128, 2, 256], F32); o += 2048
        O = sb.at(o, [128, nt], F32); o += nt * 4
        EA = [sb.at(o + i_ * 512, [128, 128], F32) for i_ in range(2)]; o += 1024
        BT = sb.at(o, [128, 256], F32)
        sgt = sb.at(o, [128, 256], F32); o += 1024
        kf = [sb.at(o + i_ * 1024, [128, 256], F32) for i_ in range(2)]; o += 2048
        S = [sb.at(o + i_ * 512, [128, 128], F32) for i_ in range(nch)]; o += nch * 512
        S2 = [sb.at(o + i_ * 512, [128, 128], F32) for i_ in range(nch)]; o += nch * 512
        RING = 4
        SBF = [[sb.at(o + (i_ * RING + k_) * 256, [128, 128], BF16) for k_ in range(RING)] for i_ in range(nch)]; o += nch * RING * 256
        STm = [[sb.at(o + (i_ * RING + k_) * 128, [128, 64], BF16) for k_ in range(RING)] for i_ in range(nch)]; o += nch * RING * 128
        sq = sb.at(o, [128, 512], BF16); o += 1024
        t1 = ntmp[0]
        rs = rstd
        MIX = sb.at(o, [128, 8, nt], BF16); o += nt * 16
        assert o <= (MWOFF if jn == "P" else LIM), o

        def prep(h):
            sx = h % 2
            X = sets[sx]
            KT, KTT, QT, V0, V1, SCAL, GS, oml = (X[k_] for k_ in ("KT", "KTT", "QT", "V0", "V1", "SCAL", "GS", "oml"))
            pr.dma("sp", lambda e: e.dma_start(out=lbt[:], in_=lb_b[h]), key="lbt", writes=["lbt"])
            pr.dve(lambda e: e.tensor_tensor(out=oml[:], in0=lbt[:, :, 0, :], in1=lbt[:, :, 1, :], op=ALU.subtract),
                   reads=["lbt"], writes=[f"oml{sx}"])
            pr.act(lambda e: e.activation(out=oml[:], in_=oml[:], func=AF.Sigmoid, scale=-1.0), reads=[f"oml{sx}"], writes=[f"oml{sx}"])
            pr.dma("pool", lambda e: e.dma_start(out=wfm[:], in_=wfm_r[h]), key="wfm", writes=["wfm"])
            pr.dma("pool", lambda e: e.dma_start(out=wtok[:], in_=wtok_r[h]), key="wtok", writes=["wtok"])
            yield
            for tt in range(ntile):
                ts = slice(tt * 512, (tt + 1) * 512)
                for i, (dst, dn_) in enumerate(((QS, "QS"), (GS, f"GS{sx}"))):
                    b = pr.next_bank()

                    def mm(e, i=i, ts=ts, b=b):
                        ins = None
                        for k in range(DC):
                            ins = e.matmul(psb(b), wfm[:, k, i * 128:(i + 1) * 128], hT[:, k, ts], start=(k == 0), stop=(k == DC - 1))
                        return ins
                    pr.pe(mm, reads=["wfm", f"hT{tt}"], writes=[f"ps{b}"])
                    pr.act(lambda e, dst=dst, ts=ts, b=b: e.activation(out=dst[:, ts], in_=psb(b), func=AF.Silu),
                           reads=[f"ps{b}"], writes=[dn_])
                    yield

            def decays(b):
                tk = slice(b * 128, (b + 1) * 128)
                q = b % 2
                ls = b % 2
                bk = pr.next_bank()

                def mm_tok(e, bk=bk):
                    ins = None
                    for dr in range(2):
                        ins = e.matmul(psb(bk)[:, dr * 128:(dr + 1) * 128], cstt[:, 396 + dr * 128:396 + (dr + 1) * 128],
                                       LFr[:, ls, dr * 128:(dr + 1) * 128], start=True, stop=True)
                    return ins
                pr.pe(mm_tok, reads=["cst", f"LF{ls}"], writes=[f"ps{bk}"])
                pr.act(lambda e, bk=bk: e.activation(out=BT[:], in_=psb(bk)[:, 0:256], func=AF.Exp), reads=[f"ps{bk}"], writes=["sgt"])
                pr.dve(lambda e: e.tensor_tensor(out=KT[:, b, :], in0=kf[q][:], in1=BT[:], op=ALU.mult),
                       reads=[f"kf{q}", "sgt"], writes=[f"KT{sx}"])
                decays_ea(b)

            def transposes(b):
                tk = slice(b * 128, (b + 1) * 128)
                bt = pr.next_bank()

                def tr(e, bt=bt):
                    ins = None
                    pv = ps_all[:, bt * 512:(bt + 1) * 512].bitcast(BF16)
                    for dr in range(2):
                        ins = e.transpose(pv[:, dr * 128:(dr + 1) * 128], KT[:, b, dr * 128:(dr + 1) * 128], ident_b[:])
                    return ins
                pr.pe(tr, reads=[f"KT{sx}", "ident"], writes=[f"ps{bt}"])

                def cpk(e, bt=bt):
                    pv = ps_all[:, bt * 512:(bt + 1) * 512].bitcast(BF16)
                    return e.tensor_copy(KTT[:, :, tk], pv[:, 0:256].rearrange("p (d t) -> p d t", d=2))
                pr.dve(cpk, reads=[f"ps{bt}"], writes=[f"KTT{sx}"])

            def decays_ea(b):
                tk = slice(b * 128, (b + 1) * 128)
                ls = b % 2
                for dr in range(2):
                    be = pr.next_bank()
                    pr.pe(lambda e, dr=dr, be=be: e.matmul(psb(be)[:, 0:134], LFr[:, ls, dr * 128:(dr + 1) * 128],
                                                           cstt[:, 128 + dr * 134:128 + (dr + 1) * 134], start=True, stop=True),
                          reads=["cst", f"LF{ls}"], writes=[f"ps{be}"])
                    pr.act(lambda e, dr=dr, be=be: e.activation(out=EA[dr][:], in_=psb(be)[:, 0:128], func=AF.Exp),
                           reads=[f"ps{be}"], writes=[f"EA{dr}"])
                    pr.act(lambda e, dr=dr, be=be: e.activation(out=SCAL[:, b, dr, :], in_=psb(be)[:, 128:134], func=AF.Exp),
                           reads=[f"ps{be}"], writes=[f"SCAL{sx}"])
                    pr.dve(lambda e, dr=dr: e.scalar_tensor_tensor(out=QT[:, dr, tk], in0=QS[:, tk], scalar=float(128 ** -0.5),
                                                                   in1=EA[dr][:], op0=ALU.mult, op1=ALU.mult),
                           reads=["QS", f"EA{dr}"], writes=[f"QT{sx}"])

            for b in range(NB):
                tk = slice(b * 128, (b + 1) * 128)
                tt = b // 4
                q = b % 2
                ls = b % 2
                bk = pr.next_bank()

                def mm(e, tk=tk, bk=bk):
                    ins = None
                    for k in range(DC):
                        ins = e.matmul(psb(bk)[:, 0:384], hT[:, k, tk], wtok[:, k, :], start=(k == 0), stop=(k == DC - 1))
                    return ins
                pr.pe(mm, reads=["wtok", f"hT{tt}"], writes=[f"ps{bk}"])
                pr.act(lambda e, bk=bk: e.activation(out=sgt[:], in_=psb(bk)[:, 0:256], func=AF.Sigmoid, scale=-1.0),
                       reads=[f"ps{bk}"], writes=["sgt"])
                pr.dve(lambda e, q=q: e.tensor_tensor(out=kf[q][:], in0=sgt[:], in1=oml[:].rearrange("p d k -> p (d k)"), op=ALU.mult),
                       reads=["sgt", f"oml{sx}"], writes=[f"kf{q}"])
                pr.act(lambda e, q=q, ls=ls: e.activation(out=LFr[:, ls, :], in_=kf[q][:], func=AF.Ln, bias=sm(SM_ONE, 1), scale=-1.0),
                       reads=[f"kf{q}", "small"], writes=[f"LF{ls}"])
                pr.dve(lambda e, b=b, bk=bk: e.tensor_copy(V0[0:64, b, :], psb(bk)[0:64, 256:384]), reads=[f"ps{bk}"], writes=[f"V{sx}"])
                pr.act(lambda e, b=b, bk=bk: e.activation(out=V1[64:128, b, :], in_=psb(bk)[64:128, 256:384], func=AF.Copy),
                       reads=[f"ps{bk}"], writes=[f"V{sx}"])
                if b >= 1:
                    decays(b - 1)
                if b >= 2:
                    transposes(b - 2)
                yield
            decays(NB - 1)
            transposes(NB - 2)
            transposes(NB - 1)
            yield

        def scan(h):
            sx = h % 2
            X = sets[sx]
            KT, KTT, QT, SCAL, GS = (X[k_] for k_ in ("KT", "KTT", "QT", "SCAL", "GS"))
            Vm = (X["V0"], X["V1"])
            chains = [(si, dr) for si in range(len(job["seqs"])) for dr in range(2)]
            pr.dve(lambda e: e.memset(O[:], 0.0), writes=["O"])
            for ci, (si, dr) in enumerate(chains):
                if jn == "S":
                    pr.dma("sp", lambda e, ci=ci, dr=dr: e.dma_start(out=S[ci][:], in_=st_in[dr, h, :, :]),
                           key=f"stin{ci}", writes=[f"S{ci}"])
                else:
                    pr.dve(lambda e, ci=ci: e.memset(S[ci][:], 0.0), writes=[f"S{ci}"])
            L = job["seqs"][0][1]
            nchunk = L // 64
            LAG = 2

            def geom(ci, step):
                si, dr = chains[ci]
                s0 = job["seqs"][si][0]
                c = step if dr == 0 else nchunk - 1 - step
                tok0 = s0 + c * 64
                b = tok0 // 128
                hf = (tok0 % 128) // 64
                return dr, b, hf, slice(tok0, tok0 + 64), slice(b * 128, (b + 1) * 128)

            def stage_a(ci, step):
                dr, b, hf, csl, tkb = geom(ci, step)
                kr = step % RING
                bs_ = pr.next_bank()
                pr.pe(lambda e: e.matmul(psb(bs_)[:, 0:64], KTT[:, dr, tkb], QT[:, dr, csl], start=True, stop=True),
                      reads=[f"KTT{sx}", f"QT{sx}"], writes=[f"ps{bs_}"])
                mo = 652 + (hf * 2 + dr) * 64
                pr.dve(lambda e: e.copy_predicated(out=STm[ci][kr][:], mask=cstt[:, mo:mo + 64].bitcast(mybir.dt.uint32),
                                                   data=psb(bs_)[:, 0:64]),
                       reads=[f"ps{bs_}", "cst"], writes=[f"ST{ci}_{kr}"])
                bp = pr.next_bank()
                pr.pe(lambda e: e.matmul(psb(bp)[:, 0:128], KT[:, b, dr * 128:(dr + 1) * 128], Vm[hf][:, b, :], start=True, stop=True),
                      reads=[f"KT{sx}", f"V{sx}"], writes=[f"ps{bp}"])
                return bp

            def stage_b(ci, step, bp):
                dr, b, hf, csl, tkb = geom(ci, step)
                kr = step % RING
                sc0 = SCAL[:, b, dr, 3 * hf + 0:3 * hf + 1]
                sc1 = SCAL[:, b, dr, 3 * hf + 1:3 * hf + 2]
                sc2 = SCAL[:, b, dr, 3 * hf + 2:3 * hf + 3]
                pr.act(lambda e: e.activation(out=SBF[ci][kr][:], in_=S[ci][:], func=AF.Copy, scale=sc0),
                       reads=[f"S{ci}", f"SCAL{sx}"], writes=[f"SBF{ci}_{kr}"])
                pr.act(lambda e: e.activation(out=S2[ci][:], in_=S[ci][:], func=AF.Copy, scale=sc1),
                       reads=[f"S{ci}", f"SCAL{sx}"], writes=[f"S2{ci}"])
                pr.dve(lambda e: e.scalar_tensor_tensor(out=S[ci][:], in0=psb(bp)[:, 0:128], scalar=sc2, in1=S2[ci][:],
                                                        op0=ALU.mult, op1=ALU.add),
                       reads=[f"ps{bp}", f"S2{ci}", f"SCAL{sx}"], writes=[f"S{ci}"])

            def stage_c(ci, step):
                dr, b, hf, csl, tkb = geom(ci, step)
                kr = step % RING
                bo = pr.next_bank()

                def mmo(e):
                    e.matmul(psb(bo)[:, 0:64], Vm[hf][:, b, :], STm[ci][kr][:], start=True, stop=False)
                    return e.matmul(psb(bo)[:, 0:64], SBF[ci][kr][:], QT[:, dr, csl], start=False, stop=True)
                pr.pe(mmo, reads=[f"V{sx}", f"ST{ci}_{kr}", f"SBF{ci}_{kr}", f"QT{sx}"], writes=[f"ps{bo}"])
                pr.dve(lambda e: e.tensor_tensor(out=O[:, csl], in0=psb(bo)[:, 0:64], in1=O[:, csl], op=ALU.add),
                       reads=[f"ps{bo}", "O"], writes=["O"])

            for step in range(nchunk + LAG):
                if step < nchunk:
                    bps = [stage_a(ci, step) for ci in range(len(chains))]
                    for ci in range(len(chains)):
                        stage_b(ci, step, bps[ci])
                if step - LAG >= 0:
                    for ci in range(len(chains)):
                        stage_c(ci, step - LAG)
                if step < nchunk:
                    yield
            if jn == "P":
                for ci, (si, dr) in enumerate(chains):
                    pr.dma("sp", lambda e, ci=ci, si=si, dr=dr: e.dma_start(out=st_out[si, dr, h, :, :], in_=S[ci][:]),
                           key=f"stout{ci}", reads=[f"S{ci}"], out=True)
            for tt in range(ntile):
                ts = slice(tt * 512, (tt + 1) * 512)
                pr.act(lambda e, ts=ts: e.activation(out=sq[:], in_=O[:, ts], func=AF.Square), reads=["O"], writes=["sq"])
                b = pr.next_bank()
                pr.pe(lambda e, b=b: e.matmul(psb(b), ones_b[:], sq[:], start=True, stop=True), reads=["sq", "ones"], writes=[f"ps{b}"])
                pr.act(lambda e, b=b: e.activation(out=rs[:], in_=psb(b), func=AF.Sqrt, bias=sm(SM_EPS, 1), scale=1.0 / 128),
                       reads=[f"ps{b}", "small"], writes=["rstd"])
                pr.dve(lambda e: e.reciprocal(rs[:], rs[:]), reads=["rstd"], writes=["rstd"])
                pr.dve(lambda e, ts=ts: e.tensor_tensor(out=t1[:], in0=O[:, ts], in1=rs[:], op=ALU.mult), reads=["O", "rstd"], writes=["ntmp0"])
                pr.dve(lambda e, ts=ts: e.scalar_tensor_tensor(out=MIX[:, h, ts], in0=t1[:], scalar=sm(SM_GN, 1), in1=GS[:, ts],
                                                               op0=ALU.mult, op1=ALU.mult),
                       reads=["ntmp0", f"GS{sx}", "small"], writes=["MIX"])
                yield

        for ci_ in range(nch):
            for k_ in range(RING):
                pr.dve(lambda e, t_=STm[ci_][k_]: e.memset(t_[:], 0.0), writes=[f"ST{ci_}_{k_}"])
        for sx_ in range(2):
            for vn_ in ("V0", "V1"):
                pr.dve(lambda e, t_=sets[sx_][vn_]: e.memset(t_[:], 0.0), writes=[f"V{sx_}"])
        g0 = prep(0)
        next(g0, None)
        if pre is not None:
            pre()
        for _ in g0:
            pass
        for h in range(A_HEADS):
            sc = scan(h)
            pp = prep(h + 1) if h + 1 < A_HEADS else None
            n_s = job["seqs"][0][1] // 64 + ntile
            n_p = (1 + 2 * ntile + NB + 1) if pp is not None else 0
            done_p = 0
            for i_s in range(n_s):
                next(sc, None)
                if jn == "P":
                    for _ in range(2):
                        if len(mod_todo) > NMC // 2:
                            mod_chunk(*mod_todo.pop(0))
                if pp is not None and i_s >= 1:
                    tgt = (n_p * i_s + n_s - 2) // (n_s - 1) if n_s > 1 else n_p
                    while done_p < min(tgt, n_p):
                        next(pp, None)
                        done_p += 1
            for _ in sc:
                pass
            if pp is not None:
                for _ in pp:
                    pass
        if jn == "P":
            while len(mod_todo) > NMC // 2:
                mod_chunk(*mod_todo.pop(0))
            mod_derived(0, 1)
            mod_derived(1, 0)
        wout0_partial(job, [w0, w0x], MIX, 0, ["wtok", "wfm"])

        pr.barrier()
        pr.phase = f"mixB_{job['name']}"
        o = SCR
        wvs = [sb.at(o + i_ * 16384, [128, DC, 512], BF16) for i_ in range(2)]; o += 32768
        VVN = sb.at(o, [128, NB, 1024], BF16); o += NB * 2048
        MIXB = sb.at(o, [128, 8, nt], BF16); o += nt * 16
        w0b = sb.at(o, [128, 8, 512], BF16); o += 8192
        vnb = sb.at(o, [128, 1024], F32); o += 4096
        bsb = sb.at(o, [128, 4, 512], F32); o += 8192
        wst = sb.at(o, [128, 4, 128], BF16); o += 1024
        gv = sb.at(o, [128, 512], F32); o += 2048
        sqv = sb.at(o, [128, 256], F32); o += 1024
        ssq = sb.at(o, [128, 8], F32); o += 32
        tmpb = sb.at(o, [128, 512], F32); o += 2048
        assert o <= LIM, o
        pr.dma("sp", lambda e: e.dma_start(out=vnb[:], in_=vn_b[:, :]), key="vnb", writes=["vnb"])
        pr.dma("sp", lambda e: e.dma_start(out=bsb[:], in_=bs_b[:, :, :]), key="bsb", writes=["bsb"])
        pr.dma("pool", lambda e: e.dma_start(out=wst[:], in_=wsT[:, :, :]), key="wst", writes=["wst"])
        for half in range(2):
            wv = wvs[half]
            wvn = f"wv{half}"
            pr.dma("pool", lambda e, half=half, wv=wv: e.dma_start(out=wv[:], in_=wvu_r[half]), key=wvn, writes=[wvn])
            for b in range(NB):
                tk = slice(b * 128, (b + 1) * 128)
                tt = b // 4
                bk = pr.next_bank()

                if jn == "P" and mod_todo:
                    mod_chunk(*mod_todo.pop(0))

                def mm(e, tk=tk, bk=bk, wv=wv):
                    ins = None
                    for k in range(DC):
                        ins = e.matmul(psb(bk), hT[:, k, tk], wv[:, k, :], start=(k == 0), stop=(k == DC - 1))
                    return ins
                pr.pe(mm, reads=[wvn, f"hT{tt}"], writes=[f"ps{bk}"])
                pr.act(lambda e, bk=bk: e.activation(out=gv[:], in_=psb(bk), func=AF.Gelu_apprx_tanh), reads=[f"ps{bk}"], writes=["gv"])
                for gi in range(2):
                    pr.act(lambda e, gi=gi: e.activation(out=sqv[:], in_=gv[:, gi * 256:(gi + 1) * 256], func=AF.Square,
                                                         accum_out=ssq[:, gi:gi + 1]),
                           reads=["gv"], writes=["sqv", "ssq"])
                pr.act(lambda e: e.activation(out=ssq[:, 2:4], in_=ssq[:, 0:2], func=AF.Sqrt, bias=sm(SM_EPS, 1), scale=1.0 / 256),
                       reads=["ssq", "small"], writes=["ssq"])
                pr.dve(lambda e: e.reciprocal(ssq[:, 4:6], ssq[:, 2:4]), reads=["ssq"], writes=["ssq"])
                for gi in range(2):
                    c0 = half * 512 + gi * 256
                    pr.dve(lambda e, gi=gi, c0=c0, b=b: e.scalar_tensor_tensor(out=VVN[:, b, c0:c0 + 256], in0=gv[:, gi * 256:(gi + 1) * 256],
                                                                               scalar=ssq[:, 4 + gi:5 + gi], in1=vnb[:, c0:c0 + 256],
                                                                               op0=ALU.mult, op1=ALU.mult),
                           reads=["gv", "ssq", "vnb"], writes=["VVN"])
        for half in range(2):
            wv = wvs[half]
            wvn = f"wv{half}"
            pr.dma("pool", lambda e, half=half, wv=wv: e.dma_start(out=wv[:], in_=wvu_r[2 + half]), key=wvn, writes=[wvn])
            for ccl in range(4):
                cc_ = half * 4 + ccl
                for tt in range(ntile):
                    ts = slice(tt * 512, (tt + 1) * 512)
                    bk = pr.next_bank()

                    if jn == "P" and mod_todo:
                        mod_chunk(*mod_todo.pop(0))

                    def mm(e, ccl=ccl, ts=ts, bk=bk, wv=wv):
                        ins = None
                        for k in range(DC):
                            ins = e.matmul(psb(bk), wv[:, k, ccl * 128:(ccl + 1) * 128], hT[:, k, ts], start=(k == 0), stop=(k == DC - 1))
                        return ins
                    pr.pe(mm, reads=[wvn, f"hT{tt}"], writes=[f"ps{bk}"])
                    pr.act(lambda e, bk=bk, cc_=cc_, ts=ts: e.activation(out=MIXB[:, cc_, ts], in_=psb(bk), func=AF.Gelu_apprx_tanh),
                           reads=[f"ps{bk}"], writes=["MIXB"])
        for cc_ in range(8):
            g = cc_ // 2
            for tt in range(ntile):
                ts = slice(tt * 512, (tt + 1) * 512)
                bk = pr.next_bank()

                if jn == "P" and mod_todo:
                    mod_chunk(*mod_todo.pop(0))

                def mm(e, cc_=cc_, tt=tt, bk=bk, g=g):
                    ins = None
                    for bl in range(4):
                        ins = e.matmul(psb(bk)[:, bl * 128:(bl + 1) * 128], VVN[:, tt * 4 + bl, cc_ * 128:(cc_ + 1) * 128], wst[:, g, :],
                                       start=True, stop=True)
                    return ins
                pr.pe(mm, reads=["VVN", "wst"], writes=[f"ps{bk}"])
                pr.dve(lambda e, bk=bk, g=g: e.tensor_tensor(out=tmpb[:], in0=psb(bk), in1=bsb[:, g, :], op=ALU.add),
                       reads=[f"ps{bk}", "bsb"], writes=["tmpb"])
                pr.dve(lambda e, cc_=cc_, ts=ts: e.tensor_tensor(out=MIXB[:, cc_, ts], in0=tmpb[:], in1=MIXB[:, cc_, ts], op=ALU.mult),
                       reads=["tmpb", "MIXB"], writes=["MIXB"])
        dbg(f"mixB_{jn}", MIXB[:], [128, 8, nt], BF16, ["MIXB"])
        dbg(f"VVN_{jn}", VVN[:], [128, NB, 1024], BF16, ["VVN"])
        if jn == "P":
            while mod_todo:
                mod_chunk(*mod_todo.pop(0))
            mod_derived(1, 1)
        w0bx = sb.at(SCR, [128, 8, 512], BF16)
        wout0_partial(job, [w0b, w0bx], MIXB, 1024, ["w0b", "wv0"], "MIXB")

    def ffn(job, l, pre=None):
        pr.barrier()
        pr.phase = f"ffn{l}_{job['name']}"
        r = job["cond"]
        ntile = job["nt"] // 512
        row = job["row"]
        nrow = 512 // row
        o = SCR
        NWUP = 4
        wup = [sb.at(o + i * 8192, [128, DC, 256], BF16) for i in range(NWUP)]; o += NWUP * 8192
        wds = [sb.at(o + i * 16384, [128, 4, D], BF16) for i in range(2)]; o += 32768
        gbuf = sb.at(o, [128, 4, 1024], BF16); o += 8192
        cg = [sb.at(o + i * 2048, [128, 512], F32) for i in range(2)]; o += 4096
        cv = [sb.at(o + i * 2048, [128, 512], F32) for i in range(2)]; o += 4096
        sg = [sb.at(o + i * 2048, [128, 512], F32) for i in range(2)]; o += 4096
        assert o <= LIM
        cwb = SM_CW[l]

        def cw_ap(j, f):
            return sm(cwb + j * 88 + f, 1)

        def cb_ap(f):
            return sm(cwb + 264 + f, 1)
        it = 0

        def load_wd(grp):
            src = ffn_dn[l][grp * 512:(grp + 1) * 512, :].rearrange("(j p) d -> p j d", p=128)
            wd_ = wds[grp % 2]
            pr.dma("pool", lambda e: e.dma_start(out=wd_[:], in_=src), key=f"wd{grp % 2}", writes=[f"wd{grp % 2}"])

        def load_wup(pair):
            src = ffn_up[l][pair]
            s_ = pair % NWUP
            pr.dma("pool", lambda e: e.dma_start(out=wup[s_][:], in_=src), key=f"wup{s_}", writes=[f"wup{s_}"])

        load_wup(0)
        load_wd(0)
        load_wup(1)
        load_wup(2)
        load_wup(3)
        if pre is not None:
            pre()
        for grp in range(NPAIR // 4):
            wd = wds[grp % 2]
            wdn = f"wd{grp % 2}"
            if grp > 0:
                load_wd(grp)
            for j in range(4):
                pair = grp * 4 + j
                slot = pair % NWUP
                if pair > 3:
                    load_wup(pair)
                for tt in range(ntile):
                    ts = slice(tt * 512, (tt + 1) * 512)
                    bg = pr.next_bank()
                    bv = pr.next_bank()

                    def mm(e, s=slot, ts=ts, bg=bg, bv=bv):
                        ins = None
                        for k in range(DC):
                            ins = e.matmul(psb(bg), wup[s][:, k, 0:128], hT[:, k, ts], start=(k == 0), stop=(k == DC - 1))
                        for k in range(DC):
                            ins = e.matmul(psb(bv), wup[s][:, k, 128:256], hT[:, k, ts], start=(k == 0), stop=(k == DC - 1))
                        return ins
                    pr.pe(mm, reads=[f"wup{slot}", f"hT{tt}"], writes=[f"ps{bg}", f"ps{bv}"])
                    q = it % 2
                    it += 1
                    fg, fv = pair, NPAIR + pair
                    pr.act(lambda e, q=q, bg=bg, fg=fg: e.activation(out=cg[q][:], in_=psb(bg), func=AF.Identity,
                                                                     bias=cb_ap(fg), scale=cw_ap(1, fg)),
                           reads=[f"ps{bg}", "small"], writes=[f"cg{q}"])
                    pr.act(lambda e, q=q, bv=bv, fv=fv: e.activation(out=cv[q][:], in_=psb(bv), func=AF.Identity,
                                                                     bias=cb_ap(fv), scale=cw_ap(1, fv)),
                           reads=[f"ps{bv}", "small"], writes=[f"cv{q}"])
                    for (buf, bn, bb, ff) in ((cg, "cg", bg, fg), (cv, "cv", bv, fv)):
                        def v3(ap):
                            return ap.rearrange("p (a b) -> p a b", a=nrow)
                        pr.dve(lambda e, buf=buf, bb=bb, ff=ff, q=q, v3=v3: e.scalar_tensor_tensor(
                            out=v3(buf[q][:])[:, :, 1:], in0=v3(psb(bb))[:, :, :row - 1], scalar=cw_ap(0, ff),
                            in1=v3(buf[q][:])[:, :, 1:], op0=ALU.mult, op1=ALU.add),
                            reads=[f"ps{bb}", f"{bn}{q}", "small"], writes=[f"{bn}{q}"])
                    for (buf, bn, bb, ff) in ((cg, "cg", bg, fg), (cv, "cv", bv, fv)):
                        def v3(ap):
                            return ap.rearrange("p (a b) -> p a b", a=nrow)
                        pr.dve(lambda e, buf=buf, bb=bb, ff=ff, q=q, v3=v3: e.scalar_tensor_tensor(
                            out=v3(buf[q][:])[:, :, :row - 1], in0=v3(psb(bb))[:, :, 1:], scalar=cw_ap(2, ff),
                            in1=v3(buf[q][:])[:, :, :row - 1], op0=ALU.mult, op1=ALU.add),
                            reads=[f"ps{bb}", f"{bn}{q}", "small"], writes=[f"{bn}{q}"])
                    pr.act(lambda e, q=q: e.activation(out=sg[q][:], in_=cg[q][:], func=AF.Silu),
                           reads=[f"cg{q}"], writes=[f"sg{q}"])
                    pr.dve(lambda e, q=q, j=j, ts=ts: e.tensor_tensor(out=gbuf[:, j, ts], in0=sg[q][:], in1=cv[q][:],
                                                                      op=ALU.mult),
                           reads=[f"sg{q}", f"cv{q}"], writes=["gbuf"])
            for dc in range(DC):
                for tt in range(ntile):
                    ts = slice(tt * 512, (tt + 1) * 512)
                    b = pr.next_bank()

                    def mm(e, dc=dc, ts=ts, b=b, wd=wd):
                        ins = None
                        for j in range(4):
                            ins = e.matmul(psb(b), wd[:, j, dc * 128:(dc + 1) * 128], gbuf[:, j, ts],
                                           start=(j == 0), stop=(j == 3))
                        return ins
                    pr.pe(mm, reads=[wdn, "gbuf"], writes=[f"ps{b}"])
                    accum_x(job, b, dc, tt, mod_ap(l, 5, r, dc))

    def fnet(job, l=1, pre=None):
        pr.barrier()
        pr.phase = f"fnet_{job['name']}"
        r = job["cond"]
        nt = job["nt"]
        ntile = nt // 512
        o = SCR
        cc = sb.at(o, [128, 4, 2, 512], BF16); o += 8192
        L = job["seqs"][0][1]
        nbs = L // 128
        dn = sb.at(o, [128, nbs, 2, L], BF16); o += nbs * 2 * L * 2
        Y = sb.at(o, [128, 8, 2, 512], BF16); o += 16384
        Z = sb.at(o, [128, 4, 1024], BF16); o += 8192
        w1 = sb.at(o, [128, 4, D], BF16); o += 16384
        assert o <= LIM, o
        pr.dma("pool", lambda e: e.dma_start(out=cc[:], in_=dftc.rearrange("(k p) s f -> p k s f", p=128)),
               key="dftc", writes=["dftc"])
        dsrc = dft256 if L == 256 else dft1024
        pr.dma("pool", lambda e: e.dma_start(out=dn[:], in_=dsrc.rearrange("(k p) s f -> p k s f", p=128)),
               key="dftn", writes=["dftn"])
        ev = 0

        def load_w1(g):
            src = w_out1[g * 512:(g + 1) * 512, :].rearrange("(j p) d -> p j d", p=128)
            pr.dma("pool", lambda e: e.dma_start(out=w1[:], in_=src), key="w1", writes=["w1"])

        load_w1(0)
        if pre is not None:
            pre()
        for g in range(4):
            if g > 0:
                load_w1(g)
            for (s0, L_) in job["seqs"]:
                for pb in range(nbs):
                    tk = slice(s0 + pb * 128, s0 + (pb + 1) * 128)
                    tt = (s0 + pb * 128) // 512
                    for cs in range(2):
                        b = pr.next_bank()

                        def mm(e, g=g, tk=tk, cs=cs, b=b):
                            ins = None
                            for kc in range(4):
                                ins = e.matmul(psb(b), hT[:, g * 4 + kc, tk], cc[:, kc, cs, :], start=(kc == 0), stop=(kc == 3))
                            return ins
                        pr.pe(mm, reads=[f"hT{tt}", "dftc"], writes=[f"ps{b}"])
                        if ev % 2 == 0:
                            pr.act(lambda e, pb=pb, cs=cs, b=b: e.activation(out=Y[:, pb, cs, :], in_=psb(b), func=AF.Copy),
                                   reads=[f"ps{b}"], writes=["Y"])
                        else:
                            pr.dve(lambda e, pb=pb, cs=cs, b=b: e.tensor_copy(Y[:, pb, cs, :], psb(b)),
                                   reads=[f"ps{b}"], writes=["Y"])
                        ev += 1
                pw = min(L_, 512)
                for cq in range(4):
                    for pt in range(L_ // pw):
                        b = pr.next_bank()

                        def mm(e, cq=cq, pt=pt, b=b, pw=pw):
                            ins = None
                            n = 0
                            for kb in range(nbs):
                                for cs in range(2):
                                    ins = e.matmul(psb(b, pw), Y[:, kb, cs, cq * 128:(cq + 1) * 128],
                                                   dn[:, kb, cs, pt * pw:(pt + 1) * pw],
                                                   start=(n == 0), stop=(n == 2 * nbs - 1))
                                    n += 1
                            return ins
                        pr.pe(mm, reads=["Y", "dftn"], writes=[f"ps{b}"])
                        zs = slice(s0 + pt * pw, s0 + (pt + 1) * pw)
                        if ev % 2 == 0:
                            pr.act(lambda e, cq=cq, zs=zs, b=b, pw=pw: e.activation(out=Z[:, cq, zs], in_=psb(b, pw), func=AF.Copy),
                                   reads=[f"ps{b}"], writes=["Z"])
                        else:
                            pr.dve(lambda e, cq=cq, zs=zs, b=b, pw=pw: e.tensor_copy(Z[:, cq, zs], psb(b, pw)),
                                   reads=[f"ps{b}"], writes=["Z"])
                        ev += 1
            for dc in range(DC):
                for tt in range(ntile):
                    ts = slice(tt * 512, (tt + 1) * 512)
                    b = pr.next_bank()

                    def mm(e, dc=dc, ts=ts, b=b):
                        ins = None
                        for j in range(4):
                            ins = e.matmul(psb(b), w1[:, j, dc * 128:(dc + 1) * 128], Z[:, j, ts], start=(j == 0), stop=(j == 3))
                        return ins
                    pr.pe(mm, reads=["w1", "Z"], writes=[f"ps{b}"])
                    accum_x(job, b, dc, tt, mod_ap(l, 2, r, dc))

    def final(job):
        pr.barrier()
        pr.phase = f"final_{job['name']}"
        ys = sb.at(SCR, [128, DC, 512], F32)
        for tt in range(job["nt"] // 512):
            norm_tile(job, tt, scale_fn=lambda c: sm(SM_GAIN + 4 * 16 + c, 1), bias_fn=lambda c: None,
                      out_fn=lambda c: ys[:, c, :], out_res="ys")
            t0 = job["t0"] + tt * 512
            dst = yT_d[:, t0:t0 + 512].rearrange("(c p) t -> p c t", p=128)
            pr.dma("sp", lambda e, dst=dst: e.dma_start(out=dst, in_=ys[:]), key="yout", reads=["ys"], out=True)

    for job in JOBS:
        nt = job["nt"]
        src = xT_d[:, job["t0"]:job["t0"] + nt].rearrange("(c p) t -> p c t", p=128)
        pr.dma("sp", lambda e, src=src, nt=nt: e.dma_start(out=xT[:, :, 0:nt], in_=src), key="xin",
               writes=[f"xT{tt}" for tt in range(nt // 512)])
        if STAGE >= 3:
            mixer0(job, pre=lambda: adaln(job, 0, 0))
        ffn(job, 0, pre=lambda: adaln(job, 0, 1))
        if STAGE >= 2:
            fnet(job, pre=lambda: adaln(job, 1, 0))
        ffn(job, 1, pre=lambda: adaln(job, 1, 1))
        final(job)

    pr.emit(nc)
    nc._dbg_names = dbg_names
    return nc


def fm(v):
    v = np.asarray(v, np.float32)
    return np.ascontiguousarray(v.reshape(-1, 128).T)


def make_consts():
    c = np.zeros((128, 1024), np.float32)
    c[:, 0:128] = np.eye(128, dtype=np.float32)
    j = np.arange(128)[:, None]
    i = np.arange(128)[None, :]
    same = (j // 64 == i // 64).astype(np.float32)
    jl = j % 64
    cmf = np.zeros((128, 134), np.float32)
    cmb = np.zeros((128, 134), np.float32)
    cmf[:, :128] = same * ((j <= i).astype(np.float32) - (jl <= 31).astype(np.float32))
    cmb[:, :128] = same * ((j >= i).astype(np.float32) - (jl >= 32).astype(np.float32))
    for ch in range(2):
        inch = (j[:, 0] // 64 == ch).astype(np.float32)
        cmf[:, 128 + 3 * ch + 0] = inch * (jl[:, 0] <= 31)
        cmf[:, 128 + 3 * ch + 1] = inch
        cmf[:, 128 + 3 * ch + 2] = inch * (jl[:, 0] > 31)
        cmb[:, 128 + 3 * ch + 0] = inch * (jl[:, 0] >= 32)
        cmb[:, 128 + 3 * ch + 1] = inch
        cmb[:, 128 + 3 * ch + 2] = inch * (jl[:, 0] < 32)
    c[:, 128:262] = cmf
    c[:, 262:396] = cmb
    c[:, 396:524] = -cmf[:, :128]
    c[:, 524:652] = -cmb[:, :128]
    ii = np.arange(64)[None, :]
    for hf in range(2):
        inh = (j // 64 == hf).astype(np.float32)
        c[:, 652 + (hf * 2 + 0) * 64:652 + (hf * 2 + 1) * 64] = inh * (jl <= ii)
        c[:, 652 + (hf * 2 + 1) * 64:652 + (hf * 2 + 2) * 64] = inh * (jl >= ii)
    return c


def dft_mats(n):
    k = np.arange(n, dtype=np.float64)
    ang = 2.0 * np.pi * np.outer(k, k) / n
    s = 1.0 / np.sqrt(n)
    return (np.cos(ang) * s), (np.sin(ang) * s)


_NC_CACHE = {}


def kernel(**inp):
    f32 = np.float32
    inp = {k: np.asarray(v) for k, v in inp.items()}
    if "nc" not in _NC_CACHE:
        _NC_CACHE["nc"] = build_program()
    nc = _NC_CACHE["nc"]

    small = np.zeros((128, 1024), f32)
    small[:, 0:96] = fm(inp["mod_b_0"])
    small[:, 96:192] = fm(inp["mod_b_1"])
    for i, nm in enumerate(["norm1_0", "norm2_0", "norm1_1", "norm2_1", "final_norm"]):
        small[:, 192 + 16 * i:192 + 16 * (i + 1)] = fm(inp[nm])
    for l in range(2):
        base = 272 + 352 * l
        cw = inp[f"ffn_conv_w_{l}"]
        for j in range(3):
            small[:, base + j * 88:base + (j + 1) * 88] = fm(cw[j])
        small[:, base + 264:base + 352] = fm(inp[f"ffn_conv_b_{l}"])
    small[:, 976] = inp["hgrn_gnorm_0"]
    small[:, 977] = EPS
    small[:, 978] = 1.0
    cst = make_consts()
    cc, sc = dft_mats(512)
    dftc = np.stack([cc, sc], axis=1).astype(f32)
    c2, s2 = dft_mats(256)
    dft256 = np.stack([c2, -s2], axis=1).astype(f32)
    c1, s1 = dft_mats(1024)
    dft1024 = np.stack([c1, -s1], axis=1).astype(f32)
    lb_b = np.ascontiguousarray(np.broadcast_to(inp["hgrn_lb"].reshape(2, 2, A_HEADS, 128).transpose(2, 0, 1, 3)[:, None], (A_HEADS, 128, 2, 2, 128))).astype(f32)
    vn_b = np.ascontiguousarray(np.broadcast_to(inp["gmlp_vnorm_0"][None], (128, 1024))).astype(f32)
    bs_b = np.ascontiguousarray(np.broadcast_to(inp["gmlp_bs_0"][None, :, None, :], (128, 4, 4, 128))).reshape(128, 4, 512).astype(f32)
    wsT = np.ascontiguousarray(inp["gmlp_ws_0"].transpose(2, 0, 1)).astype(f32)
    W = inp["w_in_0"].reshape(DC, 128, 7, A_HEADS, 128)
    wtok_r = np.ascontiguousarray(W[:, :, [1, 2, 3]].transpose(3, 1, 0, 2, 4)).reshape(A_HEADS, 128, DC, 384)
    wfm_r = np.ascontiguousarray(W[:, :, [0, 4]].transpose(3, 1, 0, 2, 4)).reshape(A_HEADS, 128, DC, 256)
    Wb = inp["w_in_0"].reshape(DC, 128, 14, 512)
    wvu_r = np.ascontiguousarray(Wb[:, :, [12, 13, 10, 11]].transpose(2, 1, 0, 3))
    shared = dict(small=small, cst=cst, dftc=dftc, dft256=dft256, dft1024=dft1024, lb_b=lb_b, vn_b=vn_b,
                  bs_b=bs_b, wsT=wsT, wtok_r=wtok_r, wfm_r=wfm_r, wvu_r=wvu_r,
                  w_out_0=inp["w_out_0"], w_out_1=inp["w_out_1"])
    for l in range(2):
        shared[f"mod_w_r_{l}"] = np.ascontiguousarray(inp[f"mod_w_{l}"].reshape(DC, 128, 6 * D // 256, 256).transpose(2, 1, 0, 3))
        U = inp[f"ffn_up_{l}"].reshape(DC, 128, 2, NPAIR, 128)
        shared[f"ffn_up_r_{l}"] = np.ascontiguousarray(U.transpose(3, 1, 0, 2, 4)).reshape(NPAIR, 128, DC, 256)
        shared[f"ffn_down_{l}"] = inp[f"ffn_down_{l}"]

    in_maps = []
    for c in range(NCORES):
        xs = np.concatenate([inp["x_prompt"][2 * c], inp["x_prompt"][2 * c + 1], inp["x_sample"][c]], axis=0)
        m = dict(shared)
        m["xT"] = np.ascontiguousarray(xs.T)
        m["st_in"] = np.ascontiguousarray(inp["state_l0_hgrn"][c])
        cond = np.stack([inp["c_ctx"], inp["c"][c]], axis=1)
        m["condT"] = np.ascontiguousarray(cond.reshape(DC, 128, 2).transpose(1, 0, 2))
        in_maps.append(m)

    res = run_bass_kernel_spmd(nc, in_maps, core_ids=list(range(NCORES)))
    if getattr(nc, "_dbg_names", None):
        _NC_CACHE["dbg"] = {k: np.asarray(res.results[0][k]).astype(np.float32) for k in nc._dbg_names}
    y_prompt = np.zeros((16, 256, D), f32)
    y_sample = np.zeros((8, 1024, D), f32)
    st_new = np.zeros((16, 2, A_HEADS, 128, 128), f32)
    for c in range(NCORES):
        y = res.results[c]["yT"].T
        y_prompt[2 * c] = y[0:256]
        y_prompt[2 * c + 1] = y[256:512]
        y_sample[c] = y[512:1536]
        st_new[2 * c:2 * c + 2] = res.results[c]["st_out"]
    return y_prompt, y_sample, st_new
```
